# Optimizing a Trainium2 kernel written in Bass

```python
import jax, jax.numpy as jnp
from jax import lax
import numpy as np

D_MODEL = 1024
BATCH = 1
SEQ = 16384
DEPTH = 4
DEC_BATCH = 32
DEC_SEQ = 32
PAST_LEN = 2048

CHUNK = 64
D_FF = 4 * D_MODEL
N_AB_LAYERS = (DEPTH + 1) // 2
N_C_LAYERS = DEPTH // 2
RET_HEADS = 4
RET_DK = D_MODEL // 16
RET_DV = D_MODEL // 8
ROPE_BASE = 10000.0
HG_HEADS = 4
HG_DK = D_MODEL // 8
HG_DV = D_MODEL // 8
AB_IN = 2 * RET_HEADS * RET_DK + 2 * RET_HEADS * RET_DV + 2 * HG_HEADS * HG_DK + 2 * HG_HEADS * HG_DV
AB_OUT = RET_HEADS * RET_DV + HG_HEADS * HG_DV
RW_N = 64
RW_HEADS = D_MODEL // RW_N
RW_DECAY_LORA = 64
RW_AAA_LORA = 64
RW_MV_LORA = 32
RW_GATE_LORA = 160
NORM_EPS = 1e-6
RW_LN_EPS = 64e-5

kernel_name = 'retention_hgrn2_rwkv7_adaln_stream_step'


def _rmsnorm(x, g):
    xf = x.astype(jnp.float32)
    y = xf * lax.rsqrt(jnp.mean(xf * xf, -1, keepdims=True) + NORM_EPS)
    return (y * g.astype(jnp.float32)).astype(x.dtype)


def _head_rms(o):
    return o * lax.rsqrt(jnp.mean(o * o, -1, keepdims=True) + NORM_EPS)


def _rotary(x, pos):
    half = x.shape[-1] // 2
    inv = ROPE_BASE ** (-jnp.arange(half, dtype=jnp.float32) / half)
    ang = pos[:, None] * inv[None, :]
    cos = jnp.cos(ang)[None, :, None, :]
    sin = jnp.sin(ang)[None, :, None, :]
    x1, x2 = x[..., :half], x[..., half:]
    return jnp.concatenate([x1 * cos - x2 * sin, x1 * sin + x2 * cos], -1)


def _chunked_scan(step, s0, *seqs):
    b, t = seqs[0].shape[:2]
    l = min(t, CHUNK)
    n = t // l
    blocks = tuple(jnp.moveaxis(a.reshape((b, n, l) + a.shape[2:]), 1, 0) for a in seqs)
    s, out = lax.scan(lambda s, xs: step(s, *xs), s0, blocks)
    out = jnp.moveaxis(out, 0, 1)
    return out.reshape((b, t) + out.shape[3:]), s


def _retention_step(s, q, k, v):
    l = q.shape[1]
    log_g = jnp.log1p(-jnp.exp2(-5.0 - jnp.arange(RET_HEADS, dtype=jnp.float32)))
    idx = jnp.arange(l, dtype=jnp.float32)
    rel = idx[:, None] - idx[None, :]
    dmat = jnp.exp(jnp.where(rel[None] >= 0, rel[None] * log_g[:, None, None], -jnp.inf))
    scores = jnp.einsum('bihd,bjhd->bhij', q, k) * dmat
    inner = jnp.einsum('bhij,bjhe->bihe', scores, v)
    cross = jnp.einsum('bihd,bhde->bihe', q, s) * jnp.exp((idx[:, None] + 1.0) * log_g[None, :])[None, :, :, None]
    k_dec = k * jnp.exp((l - 1.0 - idx)[:, None] * log_g[None, :])[None, :, :, None]
    s_new = jnp.exp(l * log_g)[None, :, None, None] * s + jnp.einsum('bjhd,bjhe->bhde', k_dec, v)
    return s_new, inner + cross


def _hgrn2_step(s, q, k, v, log_f):
    l = q.shape[1]
    b = jnp.cumsum(log_f, axis=1)
    causal = jnp.tril(jnp.ones((l, l), dtype=bool))
    diff = b[:, :, None] - b[:, None, :]
    decay = jnp.exp(jnp.where(causal[None, :, :, None, None], diff, -jnp.inf))
    attn = jnp.einsum('bihc,bjhc,bijhc->bhij', q, k, decay)
    inner = jnp.einsum('bhij,bjhe->bihe', attn, v)
    cross = jnp.einsum('bihc,bhce->bihe', q * jnp.exp(b), s)
    b_last = b[:, -1:]
    s_new = jnp.exp(b_last[:, 0])[..., None] * s + jnp.einsum('bjhc,bjhe->bhce', k * jnp.exp(b_last - b), v)
    return s_new, inner + cross


def _mix_ab(h, pos, s_ret, s_hg, w_in, lb, hg_g, w_out):
    f32 = jnp.float32
    bsz, t, _ = h.shape
    z = (h @ w_in).astype(f32)
    widths = [RET_HEADS * RET_DK] * 2 + [RET_HEADS * RET_DV] * 2 + [HG_HEADS * HG_DK] * 2 + [HG_HEADS * HG_DV] * 2
    cuts = [int(c) for c in np.cumsum(widths)[:-1]]
    q_a, k_a, v_a, g_a, q_b, f_b, i_b, g_b = jnp.split(z, cuts, axis=-1)
    q = _rotary(q_a.reshape(bsz, t, RET_HEADS, RET_DK), pos) * (RET_DK ** -0.5)
    k = _rotary(k_a.reshape(bsz, t, RET_HEADS, RET_DK), pos)
    v = v_a.reshape(bsz, t, RET_HEADS, RET_DV)
    o_a, s_ret = _chunked_scan(_retention_step, s_ret.astype(f32), q, k, v)
    o_a = _head_rms(o_a).reshape(bsz, t, -1) * jax.nn.silu(g_a)
    lb = lb.reshape(HG_HEADS, HG_DK)
    zf = f_b.reshape(bsz, t, HG_HEADS, HG_DK)
    log_f = jnp.logaddexp(jnp.log(lb), jnp.log1p(-lb) + jax.nn.log_sigmoid(zf))
    k_b = (1.0 - lb) * jax.nn.sigmoid(-zf)
    q_h = jax.nn.silu(q_b).reshape(bsz, t, HG_HEADS, HG_DK)
    v_h = i_b.reshape(bsz, t, HG_HEADS, HG_DV)
    o_b, s_hg = _chunked_scan(_hgrn2_step, s_hg.astype(f32), q_h, k_b, v_h, log_f)
    o_b = (_head_rms(o_b) * hg_g.astype(f32)).reshape(bsz, t, -1) * jax.nn.sigmoid(g_b)
    out = jnp.concatenate([o_a, o_b], -1).astype(h.dtype) @ w_out
    return out, s_ret, s_hg


def _mix_rwkv(h, shift, s_wkv, v_first, vres, mu, w_rkv, w0, w1, w2, a0, a1, a2, g1, g2, k_k, k_a, r_k, ln_g, ln_b, w_out):
    f32 = jnp.float32
    bsz, t, d = h.shape
    prev = jnp.concatenate([shift[:, None].astype(h.dtype), h[:, :-1]], axis=1)
    xx = prev - h
    xr, xw, xk, xv, xa, xg = [h + xx * mu[i] for i in range(6)]
    r = (xr @ w_rkv[0]).astype(f32)
    k = (xk @ w_rkv[1]).astype(f32)
    v = (xv @ w_rkv[2]).astype(f32)
    if vres is None:
        v_first = v
    else:
        v0, v1, v2 = vres
        v = v + (v_first - v) * jax.nn.sigmoid((v0 + (xv @ v1) @ v2).astype(f32))
    w = -jax.nn.softplus(-(w0 + jnp.tanh(xw @ w1) @ w2).astype(f32)) - 0.5
    decay = jnp.exp(-jnp.exp(w))
    a = jax.nn.sigmoid((a0 + (xa @ a1) @ a2).astype(f32))
    g = jax.nn.sigmoid(xg @ g1) @ g2
    heads = lambda u: u.reshape(bsz, t, RW_HEADS, RW_N)
    kk = heads(k * k_k)
    kk = kk / jnp.maximum(jnp.sqrt(jnp.sum(kk * kk, -1, keepdims=True)), 1e-12)
    k_h = heads(k * (1.0 + (a - 1.0) * k_a))
    r_h, v_h, w_h, a_h = heads(r), heads(v), heads(decay), heads(a)

    def step(s, inp):
        r_t, w_t, k_t, v_t, kk_t, a_t = inp
        sa = jnp.einsum('bhij,bhj->bhi', s, -kk_t)
        s = s * w_t[:, :, None, :] + sa[..., None] * (kk_t * a_t)[:, :, None, :] + v_t[..., None] * k_t[:, :, None, :]
        return s, jnp.einsum('bhij,bhj->bhi', s, r_t)

    tm = lambda u: jnp.moveaxis(u, 1, 0)
    s_wkv, y = lax.scan(step, s_wkv.astype(f32), tuple(tm(u) for u in (r_h, w_h, k_h, v_h, kk, a_h)))
    y = tm(y)
    mean = jnp.mean(y, -1, keepdims=True)
    var = jnp.mean(jnp.square(y - mean), -1, keepdims=True)
    y = ((y - mean) * lax.rsqrt(var + RW_LN_EPS)).reshape(bsz, t, d) * ln_g + ln_b
    y = y + (jnp.sum(r_h * k_h * r_k, -1, keepdims=True) * v_h).reshape(bsz, t, d)
    out = (y * g).astype(h.dtype) @ w_out
    return out, s_wkv, h[:, -1], v_first


def _trunk(x, c, pos, s_ret, s_hg, s_wkv, s_shift, W):
    lb_all = jnp.cumsum(jax.nn.softmax(W['hg_lb'].astype(jnp.float32), axis=0), axis=0)
    lb_all = lb_all - lb_all[:1]
    cond = jax.nn.silu(c)
    new_ret, new_hg, new_wkv, new_shift = [], [], [], []
    v_first = None
    for layer in range(DEPTH):
        mod = cond @ W['mod_w'][layer] + W['mod_b'][layer]
        sh1, sc1, gt1, sh2, sc2, gt2 = [u[:, None] for u in jnp.split(mod, 6, axis=-1)]
        hmix = _rmsnorm(x, W['norm_mix_g'][layer]) * (1.0 + sc1) + sh1
        m = layer // 2
        if layer % 2 == 0:
            out, r_s, h_s = _mix_ab(hmix, pos, s_ret[m], s_hg[m], W['ab_w_in'][m], lb_all[m], W['hg_norm_g'][m], W['ab_w_out'][m])
            new_ret.append(r_s)
            new_hg.append(h_s)
        else:
            vres = None if m == 0 else (W['rw_v0'][m - 1], W['rw_v1'][m - 1], W['rw_v2'][m - 1])
            out, w_s, sh_s, v_first = _mix_rwkv(hmix, s_shift[m], s_wkv[m], v_first, vres, W['rw_mu'][m], W['rw_w_rkv'][m], W['rw_w0'][m], W['rw_w1'][m], W['rw_w2'][m], W['rw_a0'][m], W['rw_a1'][m], W['rw_a2'][m], W['rw_g1'][m], W['rw_g2'][m], W['rw_k_k'][m], W['rw_k_a'][m], W['rw_r_k'][m], W['rw_ln_g'][m], W['rw_ln_b'][m], W['rw_w_out'][m])
            new_wkv.append(w_s)
            new_shift.append(sh_s)
        x = x + gt1 * out
        hmlp = _rmsnorm(x, W['norm_mlp_g'][layer]) * (1.0 + sc2) + sh2
        x = x + gt2 * (jnp.square(jax.nn.relu(hmlp @ W['mlp_w1'][layer])) @ W['mlp_w2'][layer])
    y = _rmsnorm(x, W['final_g'])
    return (y, jnp.stack(new_ret).astype(s_ret.dtype), jnp.stack(new_hg).astype(s_hg.dtype),
            jnp.stack(new_wkv).astype(s_wkv.dtype), jnp.stack(new_shift).astype(s_shift.dtype))


def setup_inputs(seed: int = 0) -> dict:
    key = jax.random.key(seed)
    ks = iter(jax.random.split(key, 64))
    f32 = jnp.float32

    def nrm(shape, scale):
        return scale * jax.random.normal(next(ks), shape, f32)

    def uni(shape, lo, hi):
        return jax.random.uniform(next(ks), shape, f32, lo, hi)

    d = D_MODEL
    return {
        'x_prompt': nrm((BATCH, SEQ, d), 1.0),
        'x_sample': nrm((DEC_BATCH, DEC_SEQ, d), 1.0),
        'state_ret': nrm((N_AB_LAYERS, DEC_BATCH, RET_HEADS, RET_DK, RET_DV), 0.5),
        'state_hgrn': nrm((N_AB_LAYERS, DEC_BATCH, HG_HEADS, HG_DK, HG_DV), 0.5),
        'state_wkv': nrm((N_C_LAYERS, DEC_BATCH, RW_HEADS, RW_N, RW_N), 0.3),
        'state_shift': nrm((N_C_LAYERS, DEC_BATCH, d), 1.0),
        'c_prompt': nrm((BATCH, d), 1.0),
        'c_sample': nrm((DEC_BATCH, d), 1.0),
        'mod_w': nrm((DEPTH, d, 6 * d), 0.5 * d ** -0.5),
        'mod_b': nrm((DEPTH, 6 * d), 0.02),
        'norm_mix_g': 1.0 + nrm((DEPTH, d), 0.05),
        'norm_mlp_g': 1.0 + nrm((DEPTH, d), 0.05),
        'final_g': 1.0 + nrm((d,), 0.05),
        'mlp_w1': nrm((DEPTH, d, D_FF), d ** -0.5),
        'mlp_w2': nrm((DEPTH, D_FF, d), D_FF ** -0.5),
        'ab_w_in': nrm((N_AB_LAYERS, d, AB_IN), d ** -0.5),
        'ab_w_out': nrm((N_AB_LAYERS, AB_OUT, d), AB_OUT ** -0.5),
        'hg_lb': nrm((N_AB_LAYERS, HG_HEADS * HG_DK), 1.0),
        'hg_norm_g': 1.0 + nrm((N_AB_LAYERS, HG_DV), 0.05),
        'rw_mu': uni((N_C_LAYERS, 6, d), 0.0, 1.0),
        'rw_w_rkv': nrm((N_C_LAYERS, 3, d, d), d ** -0.5),
        'rw_w0': uni((N_C_LAYERS, d), -4.0, 0.5),
        'rw_w1': nrm((N_C_LAYERS, d, RW_DECAY_LORA), d ** -0.5),
        'rw_w2': nrm((N_C_LAYERS, RW_DECAY_LORA, d), 0.1 * RW_DECAY_LORA ** -0.5),
        'rw_a0': nrm((N_C_LAYERS, d), 0.5),
        'rw_a1': nrm((N_C_LAYERS, d, RW_AAA_LORA), d ** -0.5),
        'rw_a2': nrm((N_C_LAYERS, RW_AAA_LORA, d), 0.1 * RW_AAA_LORA ** -0.5),
        'rw_v0': 1.0 + nrm((N_C_LAYERS - 1, d), 0.1),
        'rw_v1': nrm((N_C_LAYERS - 1, d, RW_MV_LORA), d ** -0.5),
        'rw_v2': nrm((N_C_LAYERS - 1, RW_MV_LORA, d), 0.1 * RW_MV_LORA ** -0.5),
        'rw_g1': nrm((N_C_LAYERS, d, RW_GATE_LORA), d ** -0.5),
        'rw_g2': nrm((N_C_LAYERS, RW_GATE_LORA, d), RW_GATE_LORA ** -0.5),
        'rw_k_k': 0.85 + nrm((N_C_LAYERS, d), 0.05),
        'rw_k_a': 1.0 + nrm((N_C_LAYERS, d), 0.05),
        'rw_r_k': nrm((N_C_LAYERS, RW_HEADS, RW_N), 0.1),
        'rw_ln_g': 1.0 + nrm((N_C_LAYERS, d), 0.05),
        'rw_ln_b': nrm((N_C_LAYERS, d), 0.02),
        'rw_w_out': nrm((N_C_LAYERS, d, d), d ** -0.5),
    }


def reference(x_prompt, x_sample, state_ret, state_hgrn, state_wkv, state_shift, c_prompt, c_sample,
              mod_w, mod_b, norm_mix_g, norm_mlp_g, final_g, mlp_w1, mlp_w2, ab_w_in, ab_w_out, hg_lb, hg_norm_g,
              rw_mu, rw_w_rkv, rw_w0, rw_w1, rw_w2, rw_a0, rw_a1, rw_a2, rw_v0, rw_v1, rw_v2, rw_g1, rw_g2,
              rw_k_k, rw_k_a, rw_r_k, rw_ln_g, rw_ln_b, rw_w_out):
    W = dict(mod_w=mod_w, mod_b=mod_b, norm_mix_g=norm_mix_g, norm_mlp_g=norm_mlp_g, final_g=final_g,
             mlp_w1=mlp_w1, mlp_w2=mlp_w2, ab_w_in=ab_w_in, ab_w_out=ab_w_out, hg_lb=hg_lb, hg_norm_g=hg_norm_g,
             rw_mu=rw_mu, rw_w_rkv=rw_w_rkv, rw_w0=rw_w0, rw_w1=rw_w1, rw_w2=rw_w2, rw_a0=rw_a0, rw_a1=rw_a1,
             rw_a2=rw_a2, rw_v0=rw_v0, rw_v1=rw_v1, rw_v2=rw_v2, rw_g1=rw_g1, rw_g2=rw_g2, rw_k_k=rw_k_k,
             rw_k_a=rw_k_a, rw_r_k=rw_r_k, rw_ln_g=rw_ln_g, rw_ln_b=rw_ln_b, rw_w_out=rw_w_out)
    f32 = jnp.float32
    bp = x_prompt.shape[0]
    dt = x_prompt.dtype
    pos_p = jnp.arange(x_prompt.shape[1], dtype=f32)
    pos_s = PAST_LEN + jnp.arange(x_sample.shape[1], dtype=f32)
    z_ret = jnp.zeros((N_AB_LAYERS, bp, RET_HEADS, RET_DK, RET_DV), dt)
    z_hg = jnp.zeros((N_AB_LAYERS, bp, HG_HEADS, HG_DK, HG_DV), dt)
    z_wkv = jnp.zeros((N_C_LAYERS, bp, RW_HEADS, RW_N, RW_N), dt)
    z_sh = jnp.zeros((N_C_LAYERS, bp, D_MODEL), dt)
    y_p, ret_p, hg_p, wkv_p, sh_p = _trunk(x_prompt, c_prompt, pos_p, z_ret, z_hg, z_wkv, z_sh, W)
    y_s, ret_s, hg_s, wkv_s, sh_s = _trunk(x_sample, c_sample, pos_s, state_ret, state_hgrn, state_wkv, state_shift, W)
    return (y_p, y_s, ret_p, ret_s, hg_p, hg_s, wkv_p, wkv_s, sh_p, sh_s)
```

```python
import contextlib
import numpy as np
import ml_dtypes
import concourse.bass as bass
import concourse.mybir as mybir
from concourse.bass_utils import run_bass_kernel_spmd

F32 = mybir.dt.float32
BF16 = mybir.dt.bfloat16
AF = mybir.ActivationFunctionType
ALU = mybir.AluOpType

NCORES = 8
D = 1024
DEPTH = 4
TP = 2048
NSQ = 4
TS = 32
T = TP + NSQ * TS
PAST = 2048
EPS = 1e-6
LN_EPS = 64e-5
CH = 32


class Prog:
    NDMA = 8
    SEMCHUNK = 12000
    SAME_ENG_WINDOW = 4

    def __init__(self, nc, es):
        self.nc = nc
        self.es = es
        self.engs = {'pe': nc.tensor, 'act': nc.scalar, 'dve': nc.vector, 'pool': nc.gpsimd, 'sp': nc.sync}
        self.ev = []
        self.last_w = {}
        self.readers = {}
        self.last_eng = {}
        self.dma_since_barrier = []
        self.engsems = {e: [] for e in self.engs}
        self.cnt = {e: 0 for e in self.engs}
        self.dsems = {}
        self.dcount = {}
        self.ccsem = es.enter_context(nc.semaphore("s_cc"))
        self.cccount = 0
        self.waited = {}
        self.ninstr = {e: 0 for e in self.engs}
        self.nwait = {}

    def _wait(self, engname, sem, val):
        key = (engname, id(sem))
        if self.waited.get(key, 0) >= val:
            return
        self.engs[engname].wait_ge(sem, val)
        self.nwait[engname] = self.nwait.get(engname, 0) + 1
        self.waited[key] = val

    def op(self, eng, fn, r=(), w=(), kind='c', extra=()):
        idx = len(self.ev)
        deps = set(extra)
        raw = set()
        for k in r:
            if k in self.last_w:
                deps.add(self.last_w[k])
                raw.add(self.last_w[k])
        for k in w:
            if k in self.last_w:
                deps.add(self.last_w[k])
            rd = self.readers.get(k)
            if rd:
                deps.update(rd[0].values())
                deps.update(rd[1])
        for k in w:
            self.last_w[k] = idx
            self.readers[k] = ({}, [])
        for k in r:
            rd = self.readers.setdefault(k, ({}, []))
            if kind == 'c':
                rd[0][eng] = idx
            else:
                rd[1].append(idx)
        e = self.engs[eng]
        for d in sorted(deps):
            evd = self.ev[d]
            if evd is None:
                continue
            dk, de, ds, dv, dseq = evd
            if dk == 'c' and de == eng and kind == 'c':
                if eng == 'pe' or d not in raw or self.cnt[eng] - dseq > self.SAME_ENG_WINDOW:
                    continue
            self._wait(eng, ds, dv)
        if fn is None:
            self.ev.append(None)
            return idx
        self.ninstr[eng] += 1
        if kind == 'd':
            if eng not in self.dsems:
                self.dsems[eng] = [self.es.enter_context(self.nc.semaphore("s_d%s_%d" % (eng, j))) for j in range(self.NDMA)]
                self.dcount[eng] = 0
            n = self.dcount[eng]
            self.dcount[eng] += 1
            slot, gen = n % self.NDMA, n // self.NDMA
            s = self.dsems[eng][slot]
            if gen > 0:
                self._wait(eng, s, 16 * gen)
            fn(e).then_inc(s, 16)
            self.ev.append(('d', eng, s, 16 * (gen + 1), 0))
            self.dma_since_barrier.append(idx)
        elif kind == 'cc':
            self.cccount += 1
            fn(e).then_inc(self.ccsem, 1)
            self.ev.append(('cc', eng, self.ccsem, self.cccount, 0))
            self.dma_since_barrier.append(idx)
        else:
            c = self.cnt[eng]
            si = c // self.SEMCHUNK
            while len(self.engsems[eng]) <= si:
                self.engsems[eng].append(self.es.enter_context(self.nc.semaphore("s_%s_%d" % (eng, len(self.engsems[eng])))))
            s = self.engsems[eng][si]
            fn(e).then_inc(s, 1)
            self.cnt[eng] += 1
            self.ev.append(('c', eng, s, c % self.SEMCHUNK + 1, c))
            self.last_eng[eng] = idx
        return idx

    def dma(self, out, in_, r=(), w=(), eng='sp', **kw):
        return self.op(eng, lambda e: e.dma_start(out=out, in_=in_, **kw), r=r, w=w, kind='d')

    def barrier(self):
        best = {}
        for d in self.dma_since_barrier:
            _, _, s_, v_, _ = self.ev[d]
            if id(s_) not in best or self.ev[best[id(s_)]][3] < v_:
                best[id(s_)] = d
        deps = set(self.last_eng.values()) | set(best.values())
        self.dma_since_barrier = []
        for eng in ('pe', 'act', 'dve', 'pool', 'sp'):
            self.op(eng, None, extra=deps)

    def emit(self):
        for en, sems in self.dsems.items():
            n = self.dcount[en]
            for slot, s in enumerate(sems):
                k = (n - slot + self.NDMA - 1) // self.NDMA
                if k > 0:
                    self._wait('sp', s, 16 * k)
        if self.cccount:
            self._wait('sp', self.ccsem, self.cccount)
        self.stats = dict(self.ninstr)
        self.stats["waits"] = dict(self.nwait)


def cond_ranges(t0, n):
    out = []
    t = t0
    end = t0 + n
    while t < end:
        if t < TP:
            hi = min(end, TP)
            out.append((t, hi, 0))
        else:
            q = (t - TP) // TS
            hi = min(end, TP + (q + 1) * TS)
            out.append((t, hi, 1 + q))
        t = hi
    return out


class StopBuild(Exception):
    pass


def build_program(debug=None):
    nc = bass.Bass("TRN2", target_bir_lowering=False)
    es = contextlib.ExitStack()
    P = Prog(nc, es)
    P.in_names = []
    P.dumps = []

    def din(name, shape, dt=F32):
        P.in_names.append(name)
        return nc.dram_tensor(name, list(shape), dt, kind="ExternalInput").ap()

    def dump(name, ap, shape, key, dt=F32):
        if not debug:
            return
        d_ = nc.dram_tensor("dbg_" + name, list(shape), dt, kind="ExternalOutput").ap()
        P.dma(d_, ap, r=[key], w=["dbg_" + name])
        P.dumps.append("dbg_" + name)

    def dout(name, shape, dt=F32):
        return nc.dram_tensor(name, list(shape), dt, kind="ExternalOutput").ap()

    def dscr(name, shape, dt=F32):
        return nc.dram_tensor(name, list(shape), dt).ap()

    xT = din("xT", [128, 8, T])
    condT = din("condT", [128, 8, 5])
    st_ret = din("st_ret", [2, NSQ, 4, 64, 128])
    st_hg = din("st_hg", [2, NSQ, 4, 128, 128])
    st_wkv = din("st_wkv", [2, NSQ, 16, 64, 64])
    st_shT = din("st_shT", [2, NSQ, 8, 128])
    mod_w = din("mod_w", [4, 1024, 6144])
    mod_b = din("mod_b", [192, 128])
    norm_mix_g = din("norm_mix_g", [32, 128])
    norm_mlp_g = din("norm_mlp_g", [32, 128])
    final_g = din("final_g", [8, 128])
    mlp_w1 = din("mlp_w1", [4, 1024, 4096])
    mlp_w2 = din("mlp_w2", [4, 4096, 1024])
    ab_w_in = din("ab_w_in", [2, 1024, 3584])
    ab_w_out = din("ab_w_out", [2, 1024, 1024])
    hg_lb = din("hg_lb", [8, 128])
    hg_norm_g = din("hg_norm_g", [2, 128])
    rw_mu = din("rw_mu", [96, 128])
    rw_w_rkv = din("rw_w_rkv", [2, 3, 1024, 1024])
    rw_w0 = din("rw_w0", [2, 1024]); rw_w1 = din("rw_w1", [2, 1024, 64]); rw_w2 = din("rw_w2", [2, 64, 1024])
    rw_a0 = din("rw_a0", [2, 1024]); rw_a1 = din("rw_a1", [2, 1024, 64]); rw_a2 = din("rw_a2", [2, 64, 1024])
    rw_v0 = din("rw_v0", [1, 1024]); rw_v1 = din("rw_v1", [1, 1024, 32]); rw_v2 = din("rw_v2", [1, 32, 1024])
    rw_g1 = din("rw_g1", [2, 1024, 160]); rw_g2 = din("rw_g2", [2, 160, 1024])
    rw_k_k = din("rw_k_k", [2, 1024]); rw_k_a = din("rw_k_a", [2, 1024]); rw_r_k = din("rw_r_k", [2, 1024])
    rw_ln_g = din("rw_ln_g", [2, 1024]); rw_ln_b = din("rw_ln_b", [2, 1024])
    rw_w_out = din("rw_w_out", [2, 1024, 1024])
    c_msk = din("c_msk", [128, 8, 128]); c_utn = din("c_utn", [128, 3, 128]); c_oh = din("c_oh", [128, 2])
    c_identf = din("c_identf", [128, 128])
    c_identb = din("c_identb", [128, 128], BF16)
    c_onesb = din("c_onesb", [128, 128], BF16)
    c_perm = din("c_perm", [64, 64])
    c_cos = din("c_cos", [64, T])
    c_sin = din("c_sin", [64, T])
    c_mask8 = din("c_mask8", [32, 8, 32])
    c_gq = din("c_gq", [64, 4, 32])
    c_gk = din("c_gk", [32, 4, 64])
    c_g32 = din("c_g32", [64, 4, 128])
    c_gn = din("c_gn", [64, 4, 64])
    c_gtot = din("c_gtot", [64, 4, 128])
    c_sel = din("c_sel", [128, 8])
    c_selp = din("c_selp", [128, 8])
    c_ones = din("c_ones", [128, 512])
    yT = dout("yT", [128, 8, T])
    o_ret_s = dout("o_ret_s", [2, NSQ, 4, 64, 128])
    o_hg_s = dout("o_hg_s", [2, NSQ, 4, 128, 128])
    o_wkv_s = dout("o_wkv_s", [2, NSQ, 16, 64, 64])
    o_sh_s = dout("o_sh_s", [2, NSQ, 128, 8])
    o_ret_p = dout("o_ret_p", [2, 4, 64, 128])
    o_hg_p = dout("o_hg_p", [2, 4, 128, 128])
    o_wkv_p = dout("o_wkv_p", [2, 16, 64, 64])
    o_sh_p = dout("o_sh_p", [2, 128, 8])
    s_o = dscr("s_o", [128, 8, T])
    s_qg = dscr("s_qg", [64, 4, TP], BF16)
    s_qh = dscr("s_qh", [128, 4, TP], BF16)
    ag_in = dscr("ag_in", [128, 1040])
    ag_out = dscr("ag_out", [NCORES * 128, 1040])
    ag2_in = dscr("ag2_in", [128, 8]); ag2_out = dscr("ag2_out", [NCORES * 128, 8])
    ag3_in = dscr("ag3_in", [64, 2048]); ag3_out = dscr("ag3_out", [NCORES * 64, 2048])
    s_r = dscr("s_r", [T, 1024]); s_k = dscr("s_k", [T, 1024]); s_w = dscr("s_w", [T, 1024]); s_a = dscr("s_a", [T, 1024])
    s_v = [dscr("s_v0", [T, 1024]), dscr("s_v1", [T, 1024])]
    s_vg = dscr("s_vg", [T, 1024]); s_y = dscr("s_y", [T, 1024]); s_bn = dscr("s_bn", [T, 1024])
    s_g = dscr("s_g", [128, 8, T], BF16); s_rt = dscr("s_rt", [64, 16, TP], BF16)

    _uid = [0]

    def uname(name):
        _uid[0] += 1
        return "%s_%d" % (name, _uid[0])

    sb = lambda name, shape, dt=F32: es.enter_context(nc.sbuf_tensor(uname(name), list(shape), dt))
    psb = [es.enter_context(nc.psum_tensor("psb%d" % i, [128, 512], F32)) for i in range(8)]
    PK = lambda i: ("ps", i)

    x = sb("x", [128, 8, T])
    identf = sb("identf", [128, 128])
    identb = sb("identb", [128, 128], BF16)
    onesb = sb("onesb", [128, 128], BF16)
    ones = sb("ones", [128, 512])
    modv = sb("modv", [128, 48, 5])
    A1 = sb("A1", [128, 8, 5]); A2 = sb("A2", [128, 8, 5])
    condb = sb("condb", [128, 8, 5], BF16)
    condf = sb("condf", [128, 8, 5])
    modbT = sb("modbT", [128, 192])
    gmixT = sb("gmixT", [128, 32]); gmlpT = sb("gmlpT", [128, 32]); gfinT = sb("gfinT", [128, 8])
    lbT = sb("lbT", [128, 8]); hgg = sb("hgg", [128, 2])
    sel = sb("sel", [128, 8]); selp = sb("selp", [128, 8])
    vtmp = sb("vtmp", [128, 128])
    prevc = sb("prevc", [128, 8])

    P.dma(x[:], xT[:, :, :], w=["x"])
    for t_, d_, k_ in ((identf, c_identf, "identf"), (identb, c_identb, "identb"), (onesb, c_onesb, "onesb"),
                       (ones, c_ones, "ones"), (condf, condT, "condf"), (sel, c_sel, "sel"), (selp, c_selp, "selp")):
        P.dma(t_[:], d_, w=[k_])

    def vecT(dst, dram2d, n, key):
        P.dma(vtmp[0:n, :], dram2d, w=["vtmp"])
        P.op('pe', lambda e: e.transpose(psb[7][:, 0:n], vtmp[0:n, :], identf[0:n, 0:n]), r=["vtmp", "identf"], w=[PK(7)])
        P.op('dve', lambda e: e.tensor_copy(out=dst, in_=psb[7][:, 0:n]), r=[PK(7)], w=[key])

    vecT(modbT[:, 0:128], mod_b[0:128, :], 128, "modbT")
    vecT(modbT[:, 128:192], mod_b[128:192, :], 64, "modbT")
    vecT(gmixT[:], norm_mix_g, 32, "gmixT")
    vecT(gmlpT[:], norm_mlp_g, 32, "gmlpT")
    vecT(gfinT[:], final_g, 8, "gfinT")
    vecT(lbT[:], hg_lb, 8, "lbT")
    vecT(hgg[:], hg_norm_g, 2, "hgg")
    P.op('act', lambda e: e.activation(out=condb[:], in_=condf[:], func=AF.Silu), r=["condf"], w=["condb"])
    lbv = sb("lbv", [128, 2, 4]); oml = sb("oml", [128, 2, 4])
    P.op('dve', lambda e: e.memset(lbv[:], 0.0), w=["lbv"])
    P.op('dve', lambda e: e.tensor_tensor(out=lbv[:, 1, :], in0=lbT[:, 4:8], in1=lbT[:, 0:4], op=ALU.subtract), r=["lbT"], w=["lbv"])
    P.op('act', lambda e: e.activation(out=lbv[:, 1, :], in_=lbv[:, 1, :], func=AF.Sigmoid), r=["lbv"], w=["lbv"])
    P.op('dve', lambda e: e.tensor_scalar(out=oml[:], in0=lbv[:], scalar1=-1.0, scalar2=1.0, op0=ALU.mult, op1=ALU.add), r=["lbv"], w=["oml"])

    stage = [sb("stage%d" % i, [128, 1024]) for i in range(2)]
    stage_i = [0]

    def load_w(dst, dstkey, src3, nk, ncols):
        i = stage_i[0] % 2
        stage_i[0] += 1
        st = stage[i][:, 0:nk * ncols].rearrange("p (k n) -> p k n", k=nk)
        P.dma(st, src3, w=["stage%d" % i])
        P.op('pool', lambda e: e.tensor_copy(out=dst, in_=st), r=["stage%d" % i], w=[dstkey])

    def wview(w2d, c0, ncols):
        return w2d[:, c0:c0 + ncols].rearrange("(k p) n -> p k n", p=128)

    wmod = [sb("wmod%d" % i, [128, 8, 128], BF16) for i in range(2)]

    def compute_mod(l):
        for pc in range(48):
            wb = wmod[pc % 2]
            load_w(wb[:], "wmod%d" % (pc % 2), wview(mod_w[l], pc * 128, 128), 8, 128)
            blk = pc
            for k in range(8):
                P.op('pe', (lambda o_, l_, r_, s_, t_: lambda e: e.matmul(o_, lhsT=l_, rhs=r_, start=s_, stop=t_))(
                    psb[6][:, blk * 5:blk * 5 + 5], wb[:, k, :], condb[:, k, :], k == 0, k == 7),
                    r=["wmod%d" % (pc % 2), "condb"], w=[PK(6)])
        P.op('dve', lambda e: e.tensor_tensor(out=modv[:], in0=psb[6][:, 0:240].rearrange("p (b c) -> p b c", c=5),
                                              in1=modbT[:, l * 48:(l + 1) * 48].unsqueeze(2).broadcast_to([128, 48, 5]), op=ALU.add),
             r=[PK(6), "modbT"], w=["modv"])
        for (Ax, gT, b0, key) in ((A1, gmixT, 8, "A1"), (A2, gmlpT, 32, "A2")):
            for c in range(8):
                P.op('dve', (lambda Ax_, gT_, b0_, c_: lambda e: e.tensor_scalar(out=Ax_[:, c_, :], in0=modv[:, b0_ + c_, :], scalar1=1.0, scalar2=gT_[:, l * 8 + c_:l * 8 + c_ + 1],
                                                                                 op0=ALU.add, op1=ALU.mult))(Ax, gT, b0, c),
                     r=["modv", "gmixT", "gmlpT"], w=[key])

    sqb = [sb("sqb%d" % i, [128, 512], BF16) for i in range(2)]
    rstd_t = sb("rstd_t", [128, 512])
    ntmp = [sb("ntmp%d" % i, [128, 512]) for i in range(2)]

    def norm_stats(t0, n, rstd_out, rkey):
        for c in range(8):
            sq = sqb[c % 2]
            P.op('act', (lambda sq_, c_: lambda e: e.activation(out=sq_[:, 0:n], in_=x[:, c_, t0:t0 + n], func=AF.Square))(sq, c),
                 r=["x"], w=["sqb%d" % (c % 2)])
            P.op('pe', (lambda sq_, c_: lambda e: e.matmul(psb[7][:, 0:n], lhsT=onesb[:], rhs=sq_[:, 0:n], start=c_ == 0, stop=c_ == 7))(sq, c),
                 r=["sqb%d" % (c % 2), "onesb"], w=[PK(7)])
        P.op('act', lambda e: e.activation(out=rstd_out, in_=psb[7][:, 0:n], func=AF.Ln, scale=1.0 / D, bias=EPS), r=[PK(7)], w=[rkey])
        P.op('act', lambda e: e.activation(out=rstd_out, in_=rstd_out, func=AF.Exp, scale=-0.5), r=[rkey], w=[rkey])

    def norm_apply(t0, n, rstd_ap, rkey, Ax, akey, b0, out_t, okey):
        for c in range(8):
            tm = ntmp[c % 2]
            tk = "ntmp%d" % (c % 2)
            P.op('dve', (lambda tm_, c_: lambda e: e.tensor_tensor(out=tm_[:, 0:n], in0=x[:, c_, t0:t0 + n], in1=rstd_ap, op=ALU.mult))(tm, c),
                 r=["x", rkey], w=[tk])
            for (lo, hi, ci) in cond_ranges(t0, n):
                P.op('act', (lambda tm_, c_, lo_, hi_, ci_: lambda e: e.activation(
                    out=out_t[:, c_, lo_ - t0:hi_ - t0], in_=tm_[:, lo_ - t0:hi_ - t0], func=AF.Identity,
                    bias=modv[:, b0 + c_, ci_:ci_ + 1], scale=Ax[:, c_, ci_:ci_ + 1]))(tm, c, lo, hi, ci),
                    r=[tk, "modv", akey], w=[okey])

    def resid_update(ps_ap, pskey, m, t0, n, g0):
        for (lo, hi, ci) in cond_ranges(t0, n):
            P.op('dve', (lambda lo_, hi_, ci_: lambda e: e.scalar_tensor_tensor(
                out=x[:, m, lo_:hi_], in0=ps_ap[:, lo_ - t0:hi_ - t0], scalar=modv[:, g0 + m, ci_:ci_ + 1],
                in1=x[:, m, lo_:hi_], op0=ALU.mult, op1=ALU.add))(lo, hi, ci),
                r=[pskey, "modv", "x"], w=["x"])

    def mlp_layer(l):
        with contextlib.ExitStack() as ph:
            psb_ = lambda name, shape, dt=F32: ph.enter_context(nc.sbuf_tensor(uname(name), list(shape), dt))
            hm = psb_("hm", [128, 8, 1152], BF16)
            hq = psb_("hq", [128, 8, 1152], BF16)
            w1b = [psb_("w1b%d" % i, [128, 8, 128], BF16) for i in range(2)]
            w2b = [psb_("w2b%d" % i, [128, 8, 128], BF16) for i in range(2)]
            rl = [psb_("rl%d" % i, [128, 512]) for i in range(2)]
            for (s0, sn) in ((0, 1024), (1024, 1152)):
                subs = [(s0 + o, min(512, sn - o)) for o in range(0, sn, 512)]
                for (t0, n) in subs:
                    norm_stats(t0, n, rstd_t[:, 0:n], "rstd_t")
                    norm_apply(t0, n, rstd_t[:, 0:n], "rstd_t", A2, "A2", 24, hm[:, :, t0 - s0:t0 - s0 + n], "hm")
                pi = 0
                for qtr in range(4):
                    for pc in range(8):
                        wb = w1b[pc % 2]; wk = "w1b%d" % (pc % 2)
                        load_w(wb[:], wk, wview(mlp_w1[l], qtr * 1024 + pc * 128, 128), 8, 128)
                        for b2 in range(1):
                            jl = pc
                            for (t0, n) in subs:
                                bank = pi % 2; pi += 1
                                for k in range(8):
                                    P.op('pe', (lambda o_, l_, r_, s_, t_: lambda e: e.matmul(o_, lhsT=l_, rhs=r_, start=s_, stop=t_))(
                                        psb[bank][:, 0:n], wb[:, k, b2 * 128:(b2 + 1) * 128], hm[:, k, t0 - s0:t0 - s0 + n], k == 0, k == 7),
                                        r=[wk, "hm"], w=[PK(bank)])
                                rt = rl[bank]; rk = "rl%d" % bank
                                P.op('act', (lambda rt_, b_, n_: lambda e: e.activation(out=rt_[:, 0:n_], in_=psb[b_][:, 0:n_], func=AF.Relu))(rt, bank, n),
                                     r=[PK(bank)], w=[rk])
                                P.op('dve', (lambda rt_, jl_, a_, n_: lambda e: e.tensor_tensor(out=hq[:, jl_, a_:a_ + n_], in0=rt_[:, 0:n_], in1=rt_[:, 0:n_], op=ALU.mult))(rt, jl, t0 - s0, n),
                                     r=[rk], w=["hq"])
                    for m in range(8):
                        wb = w2b[m % 2]; wk = "w2b%d" % (m % 2)
                        src = mlp_w2[l][qtr * 1024:(qtr + 1) * 1024, m * 128:(m + 1) * 128].rearrange("(j p) n -> p j n", p=128)
                        load_w(wb[:], wk, src, 8, 128)
                        for (t0, n) in subs:
                            bank = 2 + pi % 2; pi += 1
                            for j in range(8):
                                P.op('pe', (lambda o_, l_, r_, s_, t_: lambda e: e.matmul(o_, lhsT=l_, rhs=r_, start=s_, stop=t_))(
                                    psb[bank][:, 0:n], wb[:, j, :], hq[:, j, t0 - s0:t0 - s0 + n], j == 0, j == 7),
                                    r=[wk, "hq"], w=[PK(bank)])
                            resid_update(psb[bank], PK(bank), m, t0, n, 40)
        P.barrier()

    C_QA, C_KA, C_VA, C_GA, C_QB, C_FB, C_IB, C_GB = 0, 256, 512, 1024, 1536, 2048, 2560, 3072

    def ab_layer(l):
        m = l // 2
        with contextlib.ExitStack() as ph:
            S = lambda name, shape, dt=F32: ph.enter_context(nc.sbuf_tensor(uname(name), list(shape), dt))
            wA = S("wA", [128, 8, 2560], BF16)
            hmix = S("hmix", [128, 8, 128], BF16)
            qk = S("qk", [64, 8, 128])
            qkr = S("qkr", [64, 8, 128], BF16)
            cs = S("cs", [64, 2, 128])
            rt1 = S("rt1", [64, 128]); rt2 = S("rt2", [64, 128])
            perm = S("perm", [64, 64])
            va = S("va", [128, 4, 128], BF16)
            ib = S("ib", [128, 4, 128], BF16)
            qh = S("qh", [128, 4, 128], BF16)
            sig = S("sig", [128, 4, 128])
            omf = S("omf", [128, 4, 128], BF16)
            Bx = S("Bx", [128, 4, 129]); Bn = S("Bn", [128, 4, 129]); dec4 = S("dec4", [128, 4])
            eB = S("eB", [128, 4, 128])
            qhg = S("qhg", [128, 4, 128], BF16)
            qgt = S("qgt", [64, 4, 128], BF16)
            e1 = S("e1", [128, 4, 32]); e2 = S("e2", [128, 4, 32]); e3 = S("e3", [128, 4, 32])
            qt = S("qt", [128, 4, 32], BF16); kt = S("kt", [128, 4, 32], BF16); kd = S("kd", [128, 4, 32], BF16)
            qdec = S("qdec", [64, 4, 32], BF16)
            tm = S("tm", [32, 1792], BF16)
            att = S("att", [32, 8, 32], BF16)
            ot = S("ot", [128, 8, 128])
            Sret = S("Sret", [64, 4, 128]); Shg = S("Shg", [128, 4, 128])
            Sretb = S("Sretb", [64, 4, 128], BF16); Shgb = S("Shgb", [128, 4, 128], BF16)
            srt = S("srt", [64, 4, 128])
            mask8 = S("mask8", [32, 8, 32]); gq = S("gq", [64, 4, 32]); gk = S("gk", [32, 4, 64])
            g32 = S("g32", [64, 4, 128]); gn = S("gn", [64, 4, 64])
            for t_, d_, k_ in ((perm, c_perm, "perm"), (mask8, c_mask8, "mask8"), (gq, c_gq, "gq"), (gk, c_gk, "gk"),
                               (g32, c_g32, "g32"), (gn, c_gn, "gn")):
                P.dma(t_[:], d_, w=[k_])
            wcols = [(C_QA, 512, 0), (C_VA, 512, 512), (C_QB, 512, 1024), (C_FB, 512, 1536), (C_IB, 512, 2048)]
            for (c0, ncol, d0) in wcols:
                for hh in range(4):
                    load_w(wA[:, :, d0 + hh * 128:d0 + hh * 128 + 128], "wA", wview(ab_w_in[m], c0 + hh * 128, 128), 8, 128)

            def fm_proj(ps_ap, pskey, wcol0, M, n):
                for k in range(8):
                    P.op('pe', (lambda k_: lambda e: e.matmul(ps_ap, lhsT=wA[:, k_, wcol0:wcol0 + M], rhs=hmix[:, k_, 0:n], start=k_ == 0, stop=k_ == 7))(k),
                         r=["wA", "hmix"], w=[pskey])

            def run_tile(t0, n, is_prompt, seq):
                norm_stats(t0, n, rstd_t[:, 0:n], "rstd_t")
                norm_apply(t0, n, rstd_t[:, 0:n], "rstd_t", A1, "A1", 0, hmix[:, :, 0:n], "hmix")
                P.dma(cs[:, 0, 0:n], c_cos[:, t0:t0 + n], w=["cs"])
                P.dma(cs[:, 1, 0:n], c_sin[:, t0:t0 + n], w=["cs"])
                for j in range(8):
                    bank = j % 2
                    fm_proj(psb[bank][0:64, 0:n], PK(bank), j * 64, 64, n)
                    P.op('act', (lambda j_, b_: lambda e: e.activation(out=qk[:, j_, 0:n], in_=psb[b_][0:64, 0:n], func=AF.Copy))(j, bank),
                         r=[PK(bank)], w=["qk"])
                    P.op('pe', (lambda j_, b_: lambda e: e.matmul(psb[2 + b_][0:64, 0:n], lhsT=perm[:], rhs=qk[:, j_, 0:n], start=True, stop=True))(j, bank),
                         r=["qk", "perm"], w=[PK(2 + bank)])
                    P.op('dve', (lambda j_: lambda e: e.tensor_tensor(out=rt1[:, 0:n], in0=qk[:, j_, 0:n], in1=cs[:, 0, 0:n], op=ALU.mult))(j),
                         r=["qk", "cs"], w=["rt1"])
                    P.op('dve', (lambda b_: lambda e: e.tensor_tensor(out=rt2[:, 0:n], in0=psb[2 + b_][0:64, 0:n], in1=cs[:, 1, 0:n], op=ALU.mult))(bank),
                         r=[PK(2 + bank), "cs"], w=["rt2"])
                    P.op('dve', (lambda j_: lambda e: e.tensor_tensor(out=qkr[:, j_, 0:n], in0=rt1[:, 0:n], in1=rt2[:, 0:n], op=ALU.add))(j),
                         r=["rt1", "rt2"], w=["qkr"])
                for h in range(4):
                    bank = h % 2
                    fm_proj(psb[bank][:, 0:n], PK(bank), 512 + h * 128, 128, n)
                    P.op('act', (lambda h_, b_: lambda e: e.activation(out=va[:, h_, 0:n], in_=psb[b_][:, 0:n], func=AF.Copy))(h, bank), r=[PK(bank)], w=["va"])
                    fm_proj(psb[2 + bank][:, 0:n], PK(2 + bank), 2048 + h * 128, 128, n)
                    P.op('act', (lambda h_, b_: lambda e: e.activation(out=ib[:, h_, 0:n], in_=psb[2 + b_][:, 0:n], func=AF.Copy))(h, bank), r=[PK(2 + bank)], w=["ib"])
                for h in range(4):
                    bank = h % 2
                    fm_proj(psb[bank][:, 0:n], PK(bank), 1024 + h * 128, 128, n)
                    P.op('act', (lambda h_, b_: lambda e: e.activation(out=qh[:, h_, 0:n], in_=psb[b_][:, 0:n], func=AF.Silu))(h, bank), r=[PK(bank)], w=["qh"])
                    fm_proj(psb[2 + bank][:, 0:n], PK(2 + bank), 1536 + h * 128, 128, n)
                    P.op('act', (lambda h_, b_: lambda e: e.activation(out=sig[:, h_, 0:n], in_=psb[2 + b_][:, 0:n], func=AF.Sigmoid))(h, bank), r=[PK(2 + bank)], w=["sig"])
                    P.op('dve', (lambda h_: lambda e: e.tensor_scalar(out=sig[:, h_, 0:n], in0=sig[:, h_, 0:n], scalar1=oml[:, m, h_:h_ + 1], scalar2=lbv[:, m, h_:h_ + 1], op0=ALU.mult, op1=ALU.add))(h),
                         r=["sig", "oml", "lbv"], w=["sig"])
                P.op('dve', lambda e: e.tensor_scalar(out=omf[:, :, 0:n], in0=sig[:, :, 0:n], scalar1=-1.0, scalar2=1.0, op0=ALU.mult, op1=ALU.add), r=["sig"], w=["omf"])
                P.op('act', lambda e: e.activation(out=sig[:, :, 0:n], in_=sig[:, :, 0:n], func=AF.Ln), r=["sig"], w=["sig"])
                if t0 == 0 or not is_prompt:
                    P.op('dve', lambda e: e.memset(Bx[:, :, 0:1], 0.0), w=["Bx"])
                else:
                    P.op('dve', lambda e: e.tensor_copy(out=Bx[:, :, 0:1], in_=Bx[:, :, 128:129]), r=["Bx"], w=["Bx"])
                for h in range(4):
                    P.op('dve', (lambda h_: lambda e: e.tensor_tensor_scan(out=Bx[:, h_, 1:1 + n], data0=ones[:, 0:n], data1=sig[:, h_, 0:n],
                                                                          initial=Bx[:, h_, 0:1], op0=ALU.mult, op1=ALU.add))(h),
                         r=["Bx", "ones", "sig"], w=["Bx"])
                P.op('act', lambda e: e.mul(Bn[:, :, 0:n + 1], Bx[:, :, 0:n + 1], -1.0), r=["Bx"], w=["Bn"])
                if is_prompt:
                    P.op('act', lambda e: e.activation(out=eB[:, :, 0:n], in_=Bx[:, :, 1:1 + n], func=AF.Exp), r=["Bx"], w=["eB"])
                    P.op('dve', lambda e: e.tensor_tensor(out=qhg[:, :, 0:n], in0=qh[:, :, 0:n], in1=eB[:, :, 0:n], op=ALU.mult), r=["qh", "eB"], w=["qhg"])
                    P.dma(s_qh[:, :, t0:t0 + n], qhg[:, :, 0:n], r=["qhg"], w=["s_qh"])
                for c0 in range(0, n, CH):
                    cidx = (t0 + c0) // CH
                    cl = slice(c0, c0 + CH)
                    for h in range(4):
                        P.op('act', (lambda h_: lambda e: e.activation(out=e1[:, h_, :], in_=Bx[:, h_, 1 + c0:1 + c0 + CH], func=AF.Exp, bias=Bn[:, h_, c0:c0 + 1], scale=1.0))(h),
                             r=["Bx", "Bn"], w=["e1"])
                        P.op('act', (lambda h_: lambda e: e.activation(out=e2[:, h_, :], in_=Bx[:, h_, 1 + c0:1 + c0 + CH], func=AF.Exp, bias=Bx[:, h_, c0:c0 + 1], scale=-1.0))(h),
                             r=["Bx"], w=["e2"])
                        P.op('act', (lambda h_: lambda e: e.activation(out=e3[:, h_, :], in_=Bx[:, h_, 1 + c0:1 + c0 + CH], func=AF.Exp, bias=Bx[:, h_, c0 + CH:c0 + CH + 1], scale=-1.0))(h),
                             r=["Bx"], w=["e3"])
                    P.op('dve', lambda e: e.tensor_tensor(out=qt[:], in0=qh[:, :, cl], in1=e1[:], op=ALU.mult), r=["qh", "e1"], w=["qt"])
                    P.op('dve', lambda e: e.tensor_tensor(out=kt[:], in0=omf[:, :, cl], in1=e2[:], op=ALU.mult), r=["omf", "e2"], w=["kt"])
                    P.op('dve', lambda e: e.tensor_tensor(out=kd[:], in0=omf[:, :, cl], in1=e3[:], op=ALU.mult), r=["omf", "e3"], w=["kd"])
                    P.op('dve', lambda e: e.tensor_tensor(out=qdec[:], in0=qkr[:, 0:4, cl], in1=gq[:], op=ALU.mult), r=["qkr", "gq"], w=["qdec"])
                    if is_prompt:
                        P.op('dve', lambda e: e.tensor_tensor(out=qgt[:, :, cl], in0=qdec[:], in1=gn[:, :, cidx:cidx + 1].broadcast_to([64, 4, CH]), op=ALU.mult),
                             r=["qdec", "gn"], w=["qgt"])
                    pt0 = psb[4][0:32, :].bitcast(BF16)
                    pt1 = psb[5][0:32, :].bitcast(BF16)
                    for h in range(4):
                        P.op('pe', (lambda h_: lambda e: e.transpose(pt0[:, h_ * 128:(h_ + 1) * 128], kd[:, h_, :], identb[:]))(h), r=["kd", "identb"], w=[PK(4)])
                        P.op('pe', (lambda h_: lambda e: e.transpose(pt0[:, 512 + h_ * 128:512 + (h_ + 1) * 128], va[:, h_, cl], identb[:]))(h), r=["va", "identb"], w=[PK(4)])
                        P.op('pe', (lambda h_: lambda e: e.transpose(pt1[:, h_ * 128:(h_ + 1) * 128], ib[:, h_, cl], identb[:]))(h), r=["ib", "identb"], w=[PK(5)])
                        P.op('pe', (lambda h_: lambda e: e.transpose(pt1[:, 512 + h_ * 64:512 + (h_ + 1) * 64], qkr[:, 4 + h_, cl], identb[0:64, 0:64]))(h), r=["qkr", "identb"], w=[PK(5)])
                    P.op('act', lambda e: e.activation(out=tm[:, 0:1024], in_=pt0[:, 0:1024], func=AF.Copy), r=[PK(4)], w=["tm"])
                    P.op('act', lambda e: e.activation(out=tm[:, 1024:1536], in_=pt1[:, 0:512], func=AF.Copy), r=[PK(5)], w=["tm"])
                    P.op('dve', lambda e: e.tensor_tensor(out=tm[:, 1536:1792].rearrange("p (h d) -> p h d", h=4), in0=pt1[:, 512:768].rearrange("p (h d) -> p h d", h=4), in1=gk[:], op=ALU.mult),
                         r=[PK(5), "gk"], w=["tm"])
                    sc = psb[6][0:32, 0:256].rearrange("p (h i) -> p h i", h=8)
                    for h in range(4):
                        P.op('pe', (lambda h_: lambda e: e.matmul(sc[:, h_, :], lhsT=qkr[:, 4 + h_, cl], rhs=qkr[:, h_, cl], start=True, stop=True))(h), r=["qkr"], w=[PK(6)])
                        P.op('pe', (lambda h_: lambda e: e.matmul(sc[:, 4 + h_, :], lhsT=kt[:, h_, :], rhs=qt[:, h_, :], start=True, stop=True))(h), r=["kt", "qt"], w=[PK(6)])
                    P.op('dve', lambda e: e.tensor_tensor(out=att[:], in0=sc, in1=mask8[:], op=ALU.mult), r=[PK(6), "mask8"], w=["att"])
                    po = psb[7][:, 0:256].rearrange("p (h i) -> p h i", h=8)
                    for h in range(4):
                        P.op('pe', (lambda h_: lambda e: e.matmul(po[:, h_, :], lhsT=tm[:, 512 + h_ * 128:512 + (h_ + 1) * 128], rhs=att[:, h_, :], start=True, stop=False))(h), r=["tm", "att"], w=[PK(7)])
                        P.op('pe', (lambda h_: lambda e: e.matmul(po[:, h_, :], lhsT=Sretb[:, h_, :], rhs=qdec[:, h_, :], start=False, stop=True))(h), r=["Sretb", "qdec"], w=[PK(7)])
                        P.op('pe', (lambda h_: lambda e: e.matmul(po[:, 4 + h_, :], lhsT=tm[:, 1024 + h_ * 128:1024 + (h_ + 1) * 128], rhs=att[:, 4 + h_, :], start=True, stop=False))(h), r=["tm", "att"], w=[PK(7)])
                        P.op('pe', (lambda h_: lambda e: e.matmul(po[:, 4 + h_, :], lhsT=Shgb[:, h_, :], rhs=qt[:, h_, :], start=False, stop=True))(h), r=["Shgb", "qt"], w=[PK(7)])
                    P.op('act', lambda e: e.activation(out=ot[:, :, cl], in_=po, func=AF.Copy), r=[PK(7)], w=["ot"])
                    pr = psb[2][0:64, :].rearrange("p (h e) -> p h e", h=4)
                    pg = psb[3][:, :].rearrange("p (h e) -> p h e", h=4)
                    for h in range(4):
                        P.op('pe', (lambda h_: lambda e: e.matmul(pr[:, h_, :], lhsT=tm[:, 1536 + h_ * 64:1536 + (h_ + 1) * 64], rhs=tm[:, 512 + h_ * 128:512 + (h_ + 1) * 128], start=True, stop=True))(h), r=["tm"], w=[PK(2)])
                        P.op('pe', (lambda h_: lambda e: e.matmul(pg[:, h_, :], lhsT=tm[:, h_ * 128:(h_ + 1) * 128], rhs=tm[:, 1024 + h_ * 128:1024 + (h_ + 1) * 128], start=True, stop=True))(h), r=["tm"], w=[PK(3)])
                    P.op('dve', lambda e: e.tensor_tensor(out=srt[:], in0=Sret[:], in1=g32[:], op=ALU.mult), r=["Sret", "g32"], w=["srt"])
                    P.op('dve', lambda e: e.tensor_tensor(out=Sret[:], in0=srt[:], in1=pr, op=ALU.add), r=["srt", PK(2)], w=["Sret"])
                    P.op('act', lambda e: e.activation(out=Sretb[:], in_=Sret[:], func=AF.Copy), r=["Sret"], w=["Sretb"])
                    for h in range(4):
                        P.op('dve', (lambda h_: lambda e: e.scalar_tensor_tensor(out=Shg[:, h_, :], in0=Shg[:, h_, :], scalar=e1[:, h_, CH - 1:CH], in1=pg[:, h_, :], op0=ALU.mult, op1=ALU.add))(h),
                             r=["Shg", "e1", PK(3)], w=["Shg"])
                    P.op('act', lambda e: e.activation(out=Shgb[:], in_=Shg[:], func=AF.Copy), r=["Shg"], w=["Shgb"])
                if debug and t0 == debug.get("tile", 0) and l == 0:
                    dump("hmix", hmix[:, :, 0:n], [128, 8, n], "hmix", BF16)
                    dump("qkr", qkr[:, :, 0:n], [64, 8, n], "qkr", BF16)
                    dump("va", va[:, :, 0:n], [128, 4, n], "va", BF16)
                    dump("qh", qh[:, :, 0:n], [128, 4, n], "qh", BF16)
                    dump("logf", sig[:, :, 0:n], [128, 4, n], "sig")
                    dump("Bx", Bx[:, :, 0:n + 1], [128, 4, n + 1], "Bx")
                    dump("ot", ot[:, :, 0:n], [128, 8, n], "ot")
                    dump("Sret", Sret[:], [64, 4, 128], "Sret")
                    dump("Shg", Shg[:], [128, 4, 128], "Shg")
                    dump("tm", tm[:], [32, 1792], "tm", BF16)
                    dump("att", att[:], [32, 8, 32], "att", BF16)
                    dump("A1", A1[:], [128, 8, 5], "A1")
                    dump("gmixT", gmixT[:], [128, 32], "gmixT")
                    dump("qk", qk[:, :, 0:n], [64, 8, n], "qk")
                P.dma(s_o[:, :, t0:t0 + n], ot[:, :, 0:n], r=["ot"], w=["s_o"])
                if is_prompt:
                    P.dma(s_qg[:, :, t0:t0 + n], qgt[:, :, 0:n], r=["qgt"], w=["s_qg"])

            def set_state_zero():
                P.op('dve', lambda e: e.memset(Sret[:], 0.0), w=["Sret"])
                P.op('dve', lambda e: e.memset(Shg[:], 0.0), w=["Shg"])
                P.op('dve', lambda e: e.memset(Sretb[:], 0.0), w=["Sretb"])
                P.op('dve', lambda e: e.memset(Shgb[:], 0.0), w=["Shgb"])

            set_state_zero()
            for ti in range(TP // 128):
                run_tile(ti * 128, 128, True, None)
            P.dma(ag_in[0:64, 0:512], Sret[:].rearrange("p h e -> p (h e)"), r=["Sret"], w=["ag_in"])
            P.dma(ag_in[:, 512:1024], Shg[:].rearrange("p h e -> p (h e)"), r=["Shg"], w=["ag_in"])
            P.op('act', lambda e: e.activation(out=dec4[:], in_=Bx[:, :, 128], func=AF.Exp), r=["Bx"], w=["dec4"])
            P.dma(ag_in[:, 1024:1028], dec4[:], r=["dec4"], w=["ag_in"])
            P.op('pool', lambda e: e.collective_compute("AllGather", ALU.bypass, replica_groups=[list(range(NCORES))],
                                                        ins=[ag_in[:, :]], outs=[ag_out[:, :]]),
                 r=["ag_in"], w=["ag_out"], kind='cc')
            for q in range(NSQ):
                P.dma(Sret[:], st_ret[m, q].rearrange("h d e -> d h e"), w=["Sret"])
                P.dma(Shg[:], st_hg[m, q].rearrange("h c e -> c h e"), w=["Shg"])
                P.op('act', lambda e: e.activation(out=Sretb[:], in_=Sret[:], func=AF.Copy), r=["Sret"], w=["Sretb"])
                P.op('act', lambda e: e.activation(out=Shgb[:], in_=Shg[:], func=AF.Copy), r=["Shg"], w=["Shgb"])
                run_tile(TP + q * TS, TS, False, q)
                P.dma(o_ret_s[m, q].rearrange("h d e -> d h e"), Sret[:], r=["Sret"], w=["o_ret_s"])
                P.dma(o_hg_s[m, q].rearrange("h c e -> c h e"), Shg[:], r=["Shg"], w=["o_hg_s"])
        P.barrier()
        with contextlib.ExitStack() as ph:
            S = lambda name, shape, dt=F32: ph.enter_context(nc.sbuf_tensor(uname(name), list(shape), dt))
            Sr0b = S("Sr0b", [64, 512], BF16); Sh0b = S("Sh0b", [128, 512], BF16)
            hgcol = S("hgcol", [128, 8])
            with contextlib.ExitStack() as ph2:
                S2 = lambda name, shape, dt=F32: ph2.enter_context(nc.sbuf_tensor(uname(name), list(shape), dt))
                agr = S2("agr", [64, NCORES, 512]); agh = S2("agh", [128, NCORES, 512]); agd = S2("agd", [128, NCORES, 4])
                Pr = S2("Pr", [64, 512]); Ph = S2("Ph", [128, 512])
                Sr0 = S2("Sr0", [64, 512]); Sh0 = S2("Sh0", [128, 512])
                gtot = S2("gtot", [64, 512]); ftmp = S2("ftmp", [128, 512])
                P.dma(gtot[:], c_gtot.rearrange("p h e -> p (h e)"), w=["gtot"])
                ago = ag_out.rearrange("(r p) n -> p r n", p=128)
                P.dma(agr[:], ago[0:64, :, 0:512], r=["ag_out"], w=["agr"])
                P.dma(agh[:], ago[:, :, 512:1024], r=["ag_out"], w=["agh"])
                P.dma(agd[:], ago[:, :, 1024:1028], r=["ag_out"], w=["agd"])
                for wt_, k_ in ((Pr, "Pr"), (Ph, "Ph"), (Sr0, "Sr0"), (Sh0, "Sh0")):
                    P.op('dve', (lambda w_: lambda e: e.memset(w_[:], 0.0))(wt_), w=[k_])
                for r_ in range(NCORES):
                    P.op('dve', lambda e: e.scalar_tensor_tensor(out=Sr0[:], in0=Pr[:], scalar=sel[0:64, r_:r_ + 1], in1=Sr0[:], op0=ALU.mult, op1=ALU.add), r=["Pr", "sel", "Sr0"], w=["Sr0"])
                    P.op('dve', lambda e: e.scalar_tensor_tensor(out=Sh0[:], in0=Ph[:], scalar=sel[:, r_:r_ + 1], in1=Sh0[:], op0=ALU.mult, op1=ALU.add), r=["Ph", "sel", "Sh0"], w=["Sh0"])
                    P.op('dve', lambda e: e.tensor_tensor(out=ftmp[0:64, :], in0=Pr[:], in1=gtot[:], op=ALU.mult), r=["Pr", "gtot"], w=["ftmp"])
                    P.op('dve', lambda e: e.tensor_tensor(out=Pr[:], in0=ftmp[0:64, :], in1=agr[:, r_, :], op=ALU.add), r=["ftmp", "agr"], w=["Pr"])
                    for h in range(4):
                        P.op('dve', lambda e: e.scalar_tensor_tensor(out=Ph[:, h * 128:(h + 1) * 128], in0=Ph[:, h * 128:(h + 1) * 128], scalar=agd[:, r_, h:h + 1],
                                                                   in1=agh[:, r_, h * 128:(h + 1) * 128], op0=ALU.mult, op1=ALU.add), r=["Ph", "agd", "agh"], w=["Ph"])
                P.dma(o_ret_p[m].rearrange("h d e -> d h e"), Pr[:].rearrange("p (h e) -> p h e", h=4), r=["Pr"], w=["o_ret_p"])
                P.dma(o_hg_p[m].rearrange("h c e -> c h e"), Ph[:].rearrange("p (h e) -> p h e", h=4), r=["Ph"], w=["o_hg_p"])
                P.op('act', lambda e: e.activation(out=Sr0b[:], in_=Sr0[:], func=AF.Copy), r=["Sr0"], w=["Sr0b"])
                P.op('act', lambda e: e.activation(out=Sh0b[:], in_=Sh0[:], func=AF.Copy), r=["Sh0"], w=["Sh0b"])
            P.barrier()
            wB = S("wB", [128, 8, 2048], BF16)
            hmix = S("hmixB", [128, 8, 512], BF16)
            gat = S("gat", [128, 8, 512], BF16)
            ot = S("otB", [128, 8, 512])
            og = S("og", [128, 8, 512], BF16)
            qg = S("qgB", [64, 4, 512], BF16)
            qhB = S("qhB", [128, 4, 512], BF16)
            t1 = S("t1B", [128, 512]); rs = S("rsB", [128, 512]); sq = S("sqB", [128, 512], BF16)
            P.op('dve', lambda e: e.memset(hgcol[:], 1.0), w=["hgcol"])
            for h in range(4):
                P.op('dve', (lambda h_: lambda e: e.tensor_copy(out=hgcol[:, 4 + h_:5 + h_], in_=hgg[:, m:m + 1]))(h), r=["hgg"], w=["hgcol"])
            for (c0, d0) in ((C_GA, 0), (C_GB, 512)):
                for hh in range(4):
                    load_w(wB[:, :, d0 + hh * 128:d0 + hh * 128 + 128], "wB", wview(ab_w_in[m], c0 + hh * 128, 128), 8, 128)
            for hh in range(8):
                load_w(wB[:, :, 1024 + hh * 128:1024 + hh * 128 + 128], "wB", wview(ab_w_out[m], hh * 128, 128), 8, 128)
            tiles = [(i * 512, 512) for i in range(4)] + [(TP, 128)]
            for (t0, n) in tiles:
                is_prompt = t0 < TP
                norm_stats(t0, n, rstd_t[:, 0:n], "rstd_t")
                norm_apply(t0, n, rstd_t[:, 0:n], "rstd_t", A1, "A1", 0, hmix[:, :, 0:n], "hmixB")
                P.dma(ot[:, :, 0:n], s_o[:, :, t0:t0 + n], r=["s_o"], w=["otB"])
                if is_prompt:
                    P.dma(qg[:, :, 0:n], s_qg[:, :, t0:t0 + n], r=["s_qg"], w=["qgB"])
                    P.dma(qhB[:, :, 0:n], s_qh[:, :, t0:t0 + n], r=["s_qh"], w=["qhB"])
                for cc in range(8):
                    bank = cc % 2
                    for k in range(8):
                        P.op('pe', (lambda k_, cc_, b_: lambda e: e.matmul(psb[b_][:, 0:n], lhsT=wB[:, k_, cc_ * 128:(cc_ + 1) * 128], rhs=hmix[:, k_, 0:n], start=k_ == 0, stop=k_ == 7))(k, cc, bank),
                             r=["wB", "hmixB"], w=[PK(bank)])
                    P.op('act', (lambda cc_, b_: lambda e: e.activation(out=gat[:, cc_, 0:n], in_=psb[b_][:, 0:n], func=(AF.Silu if cc_ < 4 else AF.Sigmoid)))(cc, bank),
                         r=[PK(bank)], w=["gat"])
                    if is_prompt:
                        if cc < 4:
                            P.op('pe', (lambda cc_, b_: lambda e: e.matmul(psb[2 + b_][:, 0:n], lhsT=Sr0b[:, cc_ * 128:(cc_ + 1) * 128], rhs=qg[:, cc_, 0:n], start=True, stop=True))(cc, bank),
                                 r=["Sr0b", "qgB"], w=[PK(2 + bank)])
                        else:
                            P.op('pe', (lambda cc_, b_: lambda e: e.matmul(psb[2 + b_][:, 0:n], lhsT=Sh0b[:, (cc_ - 4) * 128:(cc_ - 3) * 128], rhs=qhB[:, cc_ - 4, 0:n], start=True, stop=True))(cc, bank),
                                 r=["Sh0b", "qhB"], w=[PK(2 + bank)])
                        P.op('dve', (lambda cc_, b_: lambda e: e.tensor_tensor(out=ot[:, cc_, 0:n], in0=ot[:, cc_, 0:n], in1=psb[2 + b_][:, 0:n], op=ALU.add))(cc, bank),
                             r=["otB", PK(2 + bank)], w=["otB"])
                    P.op('act', (lambda cc_: lambda e: e.activation(out=sq[:, 0:n], in_=ot[:, cc_, 0:n], func=AF.Square))(cc), r=["otB"], w=["sqB"])
                    P.op('pe', (lambda b_: lambda e: e.matmul(psb[4 + b_][:, 0:n], lhsT=onesb[:], rhs=sq[:, 0:n], start=True, stop=True))(bank), r=["sqB", "onesb"], w=[PK(4 + bank)])
                    P.op('act', (lambda b_: lambda e: e.activation(out=rs[:, 0:n], in_=psb[4 + b_][:, 0:n], func=AF.Ln, scale=1.0 / 128, bias=EPS))(bank), r=[PK(4 + bank)], w=["rsB"])
                    P.op('act', lambda e: e.activation(out=rs[:, 0:n], in_=rs[:, 0:n], func=AF.Exp, scale=-0.5), r=["rsB"], w=["rsB"])
                    P.op('dve', (lambda cc_: lambda e: e.tensor_tensor(out=t1[:, 0:n], in0=ot[:, cc_, 0:n], in1=rs[:, 0:n], op=ALU.mult))(cc), r=["otB", "rsB"], w=["t1B"])
                    P.op('dve', (lambda cc_: lambda e: e.scalar_tensor_tensor(out=og[:, cc_, 0:n], in0=t1[:, 0:n], scalar=hgcol[:, cc_:cc_ + 1], in1=gat[:, cc_, 0:n], op0=ALU.mult, op1=ALU.mult))(cc),
                         r=["t1B", "hgcol", "gat"], w=["og"])
                for mo in range(8):
                    bank = 6 + mo % 2
                    for cc in range(8):
                        P.op('pe', (lambda cc_, mo_, b_: lambda e: e.matmul(psb[b_][:, 0:n], lhsT=wB[:, cc_, 1024 + mo_ * 128:1024 + (mo_ + 1) * 128], rhs=og[:, cc_, 0:n], start=cc_ == 0, stop=cc_ == 7))(cc, mo, bank),
                             r=["wB", "og"], w=[PK(bank)])
                    resid_update(psb[bank], PK(bank), mo, t0, n, 16)
        P.barrier()

    def MM(out, lhsT, rhs, start, stop, r, w):
        P.op('pe', lambda e: e.matmul(out, lhsT=lhsT, rhs=rhs, start=start, stop=stop), r=r, w=w)

    def TRP(out, in_, ident, r, w):
        P.op('pe', lambda e: e.transpose(out, in_, ident), r=r, w=w)

    def TT(eng, out, in0, in1, op, r, w):
        P.op(eng, lambda e: e.tensor_tensor(out=out, in0=in0, in1=in1, op=op), r=r, w=w)

    def ACTF(out, in_, func, r, w, **kw):
        P.op('act', lambda e: e.activation(out=out, in_=in_, func=func, **kw), r=r, w=w)

    def rwkv_layer(l):
        m = l // 2
        has_vres = (m >= 1)
        sv_cur = s_v[m]
        with contextlib.ExitStack() as ph:
            S = lambda name, shape, dt=F32: ph.enter_context(nc.sbuf_tensor(uname(name), list(shape), dt))
            hm = S("hmR", [128, 8, 128], BF16)
            shf = S("shf", [128, 5, 8])
            agp = S("agp", [128, NCORES, 8])
            for (t0, cols) in ((TP - 128, [127]), (TP, [31, 63, 95, 127])):
                norm_stats(t0, 128, rstd_t[:, 0:128], "rstd_t")
                norm_apply(t0, 128, rstd_t[:, 0:128], "rstd_t", A1, "A1", 0, hm[:, :, 0:128], "hmR")
                for i, cidx in enumerate(cols):
                    slot = 0 if t0 < TP else 1 + i
                    ACTF(shf[:, slot, :], hm[:, :, cidx], AF.Copy, ["hmR"], ["shf"])
            P.dma(o_sh_p[m], shf[:, 0, :], r=["shf"], w=["o_sh_p"])
            for q in range(NSQ):
                P.dma(o_sh_s[m, q], shf[:, 1 + q, :], r=["shf"], w=["o_sh_s"])
            P.dma(ag2_in[:, :], shf[:, 0, :], r=["shf"], w=["ag2_in"])
            P.op('pool', lambda e: e.collective_compute("AllGather", ALU.bypass, replica_groups=[list(range(NCORES))],
                                                        ins=[ag2_in[:, :]], outs=[ag2_out[:, :]]), r=["ag2_in"], w=["ag2_out"], kind='cc')
            P.dma(agp[:], ag2_out.rearrange("(r p) n -> p r n", p=128), r=["ag2_out"], w=["agp"])
            P.op('dve', lambda e: e.memset(prevc[:], 0.0), w=["prevc"])
            for r_ in range(NCORES):
                P.op('dve', lambda e: e.scalar_tensor_tensor(out=prevc[:], in0=agp[:, r_, :], scalar=selp[:, r_:r_ + 1], in1=prevc[:], op0=ALU.mult, op1=ALU.add),
                     r=["agp", "selp", "prevc"], w=["prevc"])
        P.barrier()
        with contextlib.ExitStack() as ph:
            S = lambda name, shape, dt=F32: ph.enter_context(nc.sbuf_tensor(uname(name), list(shape), dt))
            Wrkv = S("Wrkv", [128, 8, 3072], BF16)
            Wl1 = S("Wl1", [128, 8, 320], BF16)
            w2a = S("w2a", [65, 1024]); a2a = S("a2a", [65, 1024]); v2a = S("v2a", [33, 1024])
            g2b = S("g2b", [128, 2, 1024], BF16)
            g2st = S("g2st", [128, 2, 1024])
            muT = S("muT", [128, 48])
            hx = S("hx", [128, 8, 129], BF16)
            xx = S("xx", [128, 8, 128])
            mix = [S("mix%d" % i, [128, 8, 128], BF16) for i in range(2)]
            ev = [S("ev%d" % i, [128, 1024]) for i in range(2)]
            wh = S("wh", [65, 128]); ah = S("ah", [65, 128]); vh = S("vh", [33, 128])
            gh = S("gh", [128, 2, 128], BF16)
            gfm = S("gfm", [128, 8, 128], BF16)
            vecT(muT[:], rw_mu[m * 48:(m + 1) * 48, :], 48, "muT")
            for j in range(3):
                for cb in range(8):
                    load_w(Wrkv[:, :, j * 1024 + cb * 128:j * 1024 + (cb + 1) * 128], "Wrkv", wview(rw_w_rkv[m, j], cb * 128, 128), 8, 128)
            load_w(Wl1[:, :, 0:64], "Wl1", wview(rw_w1[m], 0, 64), 8, 64)
            load_w(Wl1[:, :, 64:128], "Wl1", wview(rw_a1[m], 0, 64), 8, 64)
            load_w(Wl1[:, :, 128:256], "Wl1", wview(rw_g1[m], 0, 128), 8, 128)
            load_w(Wl1[:, :, 256:288], "Wl1", wview(rw_g1[m], 128, 32), 8, 32)
            if has_vres:
                load_w(Wl1[:, :, 288:320], "Wl1", wview(rw_v1[m - 1], 0, 32), 8, 32)
            P.dma(w2a[0:64, :], rw_w2[m], w=["w2a"]); P.dma(w2a[64:65, :], rw_w0[m:m + 1, :], w=["w2a"])
            P.dma(a2a[0:64, :], rw_a2[m], w=["a2a"]); P.dma(a2a[64:65, :], rw_a0[m:m + 1, :], w=["a2a"])
            if has_vres:
                P.dma(v2a[0:32, :], rw_v2[m - 1], w=["v2a"]); P.dma(v2a[32:33, :], rw_v0[m - 1:m, :], w=["v2a"])
            P.dma(g2st[:, 0, :], rw_g2[m][0:128, :], w=["g2st"])
            P.dma(g2st[0:32, 1, :], rw_g2[m][128:160, :], w=["g2st"])
            P.op('pool', lambda e: e.tensor_copy(out=g2b[:, 0, :], in_=g2st[:, 0, :]), r=["g2st"], w=["g2b"])
            P.op('pool', lambda e: e.tensor_copy(out=g2b[0:32, 1, :], in_=g2st[0:32, 1, :]), r=["g2st"], w=["g2b"])
            P.op('dve', lambda e: e.memset(wh[64:65, :], 1.0), w=["wh"])
            P.op('dve', lambda e: e.memset(ah[64:65, :], 1.0), w=["ah"])
            P.op('dve', lambda e: e.memset(vh[32:33, :], 1.0), w=["vh"])
            pi = [0]

            def tm_proj(mx, mxkey, wcol0, dst):
                e_ = ev[pi[0] % 2]; ek = "ev%d" % (pi[0] % 2); pi[0] += 1
                for hf in range(2):
                    for k in range(8):
                        MM(psb[hf][0:TL_, :], mx[:, k, 0:TL_], Wrkv[:, k, wcol0 + hf * 512:wcol0 + (hf + 1) * 512], k == 0, k == 7, [mxkey, "Wrkv"], [PK(hf)])
                    ACTF(e_[0:TL_, hf * 512:(hf + 1) * 512], psb[hf][0:TL_, :], AF.Copy, [PK(hf)], [ek])
                P.dma(dst, e_[0:TL_, :], r=[ek], w=["s_rkv"])

            def lora2(hid, hkey, K, w2t, w2key, dst):
                e_ = ev[pi[0] % 2]; ek = "ev%d" % (pi[0] % 2); pi[0] += 1
                for hf in range(2):
                    MM(psb[2 + hf][0:TL_, :], hid[0:K, 0:TL_], w2t[0:K, hf * 512:(hf + 1) * 512], True, True, [hkey, w2key], [PK(2 + hf)])
                    ACTF(e_[0:TL_, hf * 512:(hf + 1) * 512], psb[2 + hf][0:TL_, :], AF.Copy, [PK(2 + hf)], [ek])
                P.dma(dst, e_[0:TL_, :], r=[ek], w=["s_rkv"])

            def mk_mix(i, mi):
                mx = mix[mi % 2]; mk = "mix%d" % (mi % 2)
                for c in range(8):
                    P.op('dve', lambda e: e.scalar_tensor_tensor(out=mx[:, c, 0:TL_], in0=xx[:, c, 0:TL_], scalar=muT[:, i * 8 + c:i * 8 + c + 1], in1=hx[:, c, 1:1 + TL_], op0=ALU.mult, op1=ALU.add),
                         r=["xx", "muT", "hx"], w=[mk])
                return mx, mk

            tiles = [(i * 128, 128) for i in range(TP // 128)] + [(TP + q * TS, TS) for q in range(NSQ)]
            for ti, (t0, TL_) in enumerate(tiles):
                is_prompt = t0 < TP
                if t0 == 0:
                    P.op('dve', lambda e: e.tensor_copy(out=hx[:, :, 0], in_=prevc[:]), r=["prevc"], w=["hx"])
                elif is_prompt:
                    P.op('dve', lambda e: e.tensor_copy(out=hx[:, :, 0], in_=hx[:, :, 128]), r=["hx"], w=["hx"])
                else:
                    q = (t0 - TP) // TS
                    P.dma(xx[:, :, 0], st_shT[m, q].rearrange("c p -> p c"), w=["xx"], allow_slow_non_contiguous=True)
                    P.op('dve', lambda e: e.tensor_copy(out=hx[:, :, 0], in_=xx[:, :, 0]), r=["xx"], w=["hx"])
                norm_stats(t0, TL_, rstd_t[:, 0:TL_], "rstd_t")
                norm_apply(t0, TL_, rstd_t[:, 0:TL_], "rstd_t", A1, "A1", 0, hx[:, :, 1:1 + TL_], "hx")
                TT('dve', xx[:, :, 0:TL_], hx[:, :, 0:TL_], hx[:, :, 1:1 + TL_], ALU.subtract, ["hx"], ["xx"])
                rows = slice(t0, t0 + TL_)
                mi = 0
                mx, mk = mk_mix(0, mi); mi += 1
                tm_proj(mx, mk, 0, s_r[rows, :])
                mx, mk = mk_mix(2, mi); mi += 1
                tm_proj(mx, mk, 1024, s_k[rows, :])
                mx, mk = mk_mix(3, mi); mi += 1
                tm_proj(mx, mk, 2048, sv_cur[rows, :])
                if has_vres:
                    for k in range(8):
                        MM(psb[4][0:32, 0:TL_], Wl1[:, k, 288:320], mx[:, k, 0:TL_], k == 0, k == 7, ["Wl1", mk], [PK(4)])
                    ACTF(vh[0:32, 0:TL_], psb[4][0:32, 0:TL_], AF.Copy, [PK(4)], ["vh"])
                    lora2(vh, "vh", 33, v2a, "v2a", s_vg[rows, :])
                mx, mk = mk_mix(1, mi); mi += 1
                for k in range(8):
                    MM(psb[4][0:64, 0:TL_], Wl1[:, k, 0:64], mx[:, k, 0:TL_], k == 0, k == 7, ["Wl1", mk], [PK(4)])
                ACTF(wh[0:64, 0:TL_], psb[4][0:64, 0:TL_], AF.Tanh, [PK(4)], ["wh"])
                lora2(wh, "wh", 65, w2a, "w2a", s_w[rows, :])
                mx, mk = mk_mix(4, mi); mi += 1
                for k in range(8):
                    MM(psb[5][0:64, 0:TL_], Wl1[:, k, 64:128], mx[:, k, 0:TL_], k == 0, k == 7, ["Wl1", mk], [PK(5)])
                ACTF(ah[0:64, 0:TL_], psb[5][0:64, 0:TL_], AF.Copy, [PK(5)], ["ah"])
                lora2(ah, "ah", 65, a2a, "a2a", s_a[rows, :])
                mx, mk = mk_mix(5, mi); mi += 1
                for k in range(8):
                    MM(psb[4][:, 0:TL_], Wl1[:, k, 128:256], mx[:, k, 0:TL_], k == 0, k == 7, ["Wl1", mk], [PK(4)])
                ACTF(gh[:, 0, 0:TL_], psb[4][:, 0:TL_], AF.Sigmoid, [PK(4)], ["gh"])
                for k in range(8):
                    MM(psb[5][0:32, 0:TL_], Wl1[:, k, 256:288], mx[:, k, 0:TL_], k == 0, k == 7, ["Wl1", mk], [PK(5)])
                ACTF(gh[0:32, 1, 0:TL_], psb[5][0:32, 0:TL_], AF.Sigmoid, [PK(5)], ["gh"])
                for cb in range(8):
                    bk = 6 + cb % 2
                    MM(psb[bk][:, 0:TL_], g2b[:, 0, cb * 128:(cb + 1) * 128], gh[:, 0, 0:TL_], True, False, ["g2b", "gh"], [PK(bk)])
                    MM(psb[bk][:, 0:TL_], g2b[0:32, 1, cb * 128:(cb + 1) * 128], gh[0:32, 1, 0:TL_], False, True, ["g2b", "gh"], [PK(bk)])
                    ACTF(gfm[:, cb, 0:TL_], psb[bk][:, 0:TL_], AF.Copy, [PK(bk)], ["gfm"])
                P.dma(s_g[:, :, rows], gfm[:, :, 0:TL_], r=["gfm"], w=["s_g"])
        P.barrier()
        with contextlib.ExitStack() as ph:
            S = lambda name, shape, dt=F32: ph.enter_context(nc.sbuf_tensor(uname(name), list(shape), dt))
            kkbc = S("kkbc", [128, 512]); kabc = S("kabc", [128, 512]); rkbc = S("rkbc", [128, 512])
            Rt = S("Rt", [128, 512]); Kt = S("Kt", [128, 512]); Vt = S("Vt", [128, 512]); Wt = S("Wt", [128, 512]); At = S("At", [128, 512])
            KK = S("KK", [128, 512]); T1 = S("T1", [128, 512]); KA = S("KA", [128, 512])
            Ein = S("Ein", [128, 512]); Eni = S("Eni", [128, 512]); Eex = S("Eex", [128, 512]); EL = S("EL", [128, 512])
            VF = S("VF", [128, 512]); VG = S("VG", [128, 512])
            Ah = S("Ah", [128, 512], BF16); Bh = S("Bh", [128, 512], BF16); Kh = S("Kh", [128, 512], BF16); Rh = S("Rh", [128, 512], BF16)
            Bp = S("Bp", [128, 512], BF16); Kp = S("Kp", [128, 512], BF16)
            Vaug = S("Vaug", [128, 8, 128], BF16)
            opT = S("opT", [64, 4, 8, 128], BF16)
            ss = S("ss", [128, 8]); sbn = S("sbn", [128, 8]); ss2 = S("ss2", [128, 8])
            S0T = S("S0T", [64, 16, 128]); S0Tb = S("S0Tb", [64, 16, 128], BF16)
            WL = S("WL", [64, 16])
            PT = [S("PT%d" % i, [128, 4, 128], BF16) for i in range(2)]
            Pn = [S("Pn%d" % i, [128, 4, 128], BF16) for i in range(2)]
            NakT = S("NakT", [128, 4, 128], BF16); MrbT = S("MrbT", [128, 4, 128], BF16); MrkT = S("MrkT", [128, 4, 128], BF16)
            Xin = S("Xin", [128, 4, 128], BF16); Xb = S("Xb", [128, 4, 128], BF16)
            TA = [S("TA%d" % i, [128, 4, 128], BF16) for i in range(2)]; TB = [S("TB%d" % i, [128, 4, 128], BF16) for i in range(2)]
            Zt = [S("Zt%d" % i, [128, 4, 128], BF16) for i in range(2)]
            NoT64 = S("NoT64", [128, 4, 128], BF16); No64 = S("No64", [128, 4, 128], BF16); No128 = S("No128", [128, 4, 128], BF16)
            yt = S("ytm", [128, 8, 64]); rtT = S("rtT", [64, 8, 128], BF16)
            msk = S("msk", [128, 8, 128]); utn = S("utn", [128, 3, 128]); oh = S("oh", [128, 2])
            sin_ = S("sin_", [64, 16, 64]); sot = S("sot", [64, 16, 64])
            P.dma(msk[:], c_msk, w=["msk"]); P.dma(utn[:], c_utn, w=["utn"]); P.dma(oh[:], c_oh, w=["oh"])
            P.op('dve', lambda e: e.memset(Vaug[:], 0.0), w=["Vaug"])

            def rec_tile(t0, TL_, aug):
                Wd = 128 if aug else 64
                L = 7 if TL_ == 128 else 5
                ohc = 0 if TL_ == 128 else 1
                rows = slice(t0, t0 + TL_)
                for hf in range(2):
                    cs_ = slice(hf * 512, (hf + 1) * 512)
                    P.dma(kkbc[:], rw_k_k[m, cs_].partition_broadcast(128), w=["kkbc"])
                    P.dma(kabc[:], rw_k_a[m, cs_].partition_broadcast(128), w=["kabc"])
                    P.dma(rkbc[:], rw_r_k[m, cs_].partition_broadcast(128), w=["rkbc"])
                    P.dma(Rt[0:TL_, :], s_r[rows, cs_], r=["s_rkv"], w=["Rt"])
                    P.dma(Kt[0:TL_, :], s_k[rows, cs_], r=["s_rkv"], w=["Kt"])
                    P.dma(Vt[0:TL_, :], sv_cur[rows, cs_], r=["s_rkv"], w=["Vt"])
                    P.dma(Wt[0:TL_, :], s_w[rows, cs_], r=["s_rkv"], w=["Wt"])
                    P.dma(At[0:TL_, :], s_a[rows, cs_], r=["s_rkv"], w=["At"])
                    if has_vres:
                        P.dma(VF[0:TL_, :], s_v[0][rows, cs_], r=["s_rkv"], w=["VF"])
                        P.dma(VG[0:TL_, :], s_vg[rows, cs_], r=["s_rkv"], w=["VG"])
                        ACTF(VG[0:TL_, :], VG[0:TL_, :], AF.Sigmoid, ["VG"], ["VG"])
                        TT('dve', VF[0:TL_, :], VF[0:TL_, :], Vt[0:TL_, :], ALU.subtract, ["VF", "Vt"], ["VF"])
                        TT('dve', VF[0:TL_, :], VF[0:TL_, :], VG[0:TL_, :], ALU.mult, ["VF", "VG"], ["VF"])
                        TT('dve', Vt[0:TL_, :], Vt[0:TL_, :], VF[0:TL_, :], ALU.add, ["VF", "Vt"], ["Vt"])
                    dbg_here = bool(debug) and l == 1 and t0 == debug.get("rt0", TP) and hf == 0
                    if dbg_here:
                        dump("r_raw", Rt[0:TL_, :], [TL_, 512], "Rt"); dump("k_raw", Kt[0:TL_, :], [TL_, 512], "Kt"); dump("v_raw", Vt[0:TL_, :], [TL_, 512], "Vt")
                        dump("w_raw", Wt[0:TL_, :], [TL_, 512], "Wt"); dump("a_raw", At[0:TL_, :], [TL_, 512], "At")
                        dump("S0T_in", S0T[:], [64, 16, 128], "S0T")
                    ACTF(Wt[0:TL_, :], Wt[0:TL_, :], AF.Exp, ["Wt"], ["Wt"], scale=-1.0)
                    ACTF(Wt[0:TL_, :], Wt[0:TL_, :], AF.Ln, ["Wt"], ["Wt"], bias=1.0)
                    ACTF(Wt[0:TL_, :], Wt[0:TL_, :], AF.Exp, ["Wt"], ["Wt"], scale=-1.0, bias=-0.5)
                    for j in range(3):
                        MM(psb[j][0:TL_, :], utn[0:TL_, j, 0:TL_], Wt[0:TL_, :], True, True, ["utn", "Wt"], [PK(j)])
                    ACTF(Ein[0:TL_, :], psb[0][0:TL_, :], AF.Exp, [PK(0)], ["Ein"])
                    ACTF(Eni[0:TL_, :], psb[0][0:TL_, :], AF.Exp, [PK(0)], ["Eni"], scale=-1.0)
                    ACTF(Eex[0:TL_, :], psb[1][0:TL_, :], AF.Exp, [PK(1)], ["Eex"])
                    ACTF(EL[0:TL_, :], psb[2][0:TL_, :], AF.Exp, [PK(2)], ["EL"])
                    ACTF(At[0:TL_, :], At[0:TL_, :], AF.Sigmoid, ["At"], ["At"])
                    TT('dve', KK[0:TL_, :], Kt[0:TL_, :], kkbc[0:TL_, :], ALU.mult, ["Kt", "kkbc"], ["KK"])
                    ACTF(T1[0:TL_, :], KK[0:TL_, :], AF.Square, ["KK"], ["T1"])
                    P.op('dve', lambda e: e.tensor_reduce(out=ss[0:TL_, :], in_=T1[0:TL_, :].rearrange("p (h c) -> p h c", h=8), axis=mybir.AxisListType.X, op=ALU.add), r=["T1"], w=["ss"])
                    P.op('dve', lambda e: e.tensor_copy(out=T1[0:TL_, :].rearrange("p (h c) -> p h c", h=8), in_=ss[0:TL_, :].unsqueeze(2).broadcast_to([TL_, 8, 64])), r=["ss"], w=["T1"])
                    ACTF(T1[0:TL_, :], T1[0:TL_, :], AF.Ln, ["T1"], ["T1"])
                    ACTF(T1[0:TL_, :], T1[0:TL_, :], AF.Exp, ["T1"], ["T1"], scale=-0.5)
                    TT('dve', KK[0:TL_, :], KK[0:TL_, :], T1[0:TL_, :], ALU.mult, ["KK", "T1"], ["KK"])
                    TT('dve', KA[0:TL_, :], KK[0:TL_, :], At[0:TL_, :], ALU.mult, ["KK", "At"], ["KA"])
                    P.op('dve', lambda e: e.scalar_tensor_tensor(out=T1[0:TL_, :], in0=At[0:TL_, :], scalar=-1.0, in1=kabc[0:TL_, :], op0=ALU.add, op1=ALU.mult), r=["At", "kabc"], w=["T1"])
                    P.op('dve', lambda e: e.scalar_tensor_tensor(out=Kt[0:TL_, :], in0=T1[0:TL_, :], scalar=1.0, in1=Kt[0:TL_, :], op0=ALU.add, op1=ALU.mult), r=["T1", "Kt"], w=["Kt"])
                    TT('dve', T1[0:TL_, :], Rt[0:TL_, :], Kt[0:TL_, :], ALU.mult, ["Rt", "Kt"], ["T1"])
                    TT('dve', T1[0:TL_, :], T1[0:TL_, :], rkbc[0:TL_, :], ALU.mult, ["T1", "rkbc"], ["T1"])
                    P.op('dve', lambda e: e.tensor_reduce(out=sbn[0:TL_, :], in_=T1[0:TL_, :].rearrange("p (h c) -> p h c", h=8), axis=mybir.AxisListType.X, op=ALU.add), r=["T1"], w=["sbn"])
                    TT('dve', T1[0:TL_, :].rearrange("p (h c) -> p h c", h=8), Vt[0:TL_, :].rearrange("p (h c) -> p h c", h=8),
                       sbn[0:TL_, :].unsqueeze(2).broadcast_to([TL_, 8, 64]), ALU.mult, ["Vt", "sbn"], ["T1"])
                    P.dma(s_bn[rows, cs_], T1[0:TL_, :], r=["T1"], w=["s_bn"])
                    P.op('dve', lambda e: e.scalar_tensor_tensor(out=Ah[0:TL_, :], in0=KK[0:TL_, :], scalar=-1.0, in1=Eex[0:TL_, :], op0=ALU.mult, op1=ALU.mult), r=["KK", "Eex"], w=["Ah"])
                    TT('dve', Bh[0:TL_, :], KA[0:TL_, :], Eni[0:TL_, :], ALU.mult, ["KA", "Eni"], ["Bh"])
                    TT('dve', Kh[0:TL_, :], Kt[0:TL_, :], Eni[0:TL_, :], ALU.mult, ["Kt", "Eni"], ["Kh"])
                    TT('dve', Rh[0:TL_, :], Rt[0:TL_, :], Ein[0:TL_, :], ALU.mult, ["Rt", "Ein"], ["Rh"])
                    TT('dve', Bp[0:TL_, :], KA[0:TL_, :], EL[0:TL_, :], ALU.mult, ["KA", "EL"], ["Bp"])
                    TT('dve', Kp[0:TL_, :], Kt[0:TL_, :], EL[0:TL_, :], ALU.mult, ["Kt", "EL"], ["Kp"])
                    P.op('act', lambda e: e.activation(out=Vaug[0:TL_, :, 0:64], in_=Vt[0:TL_, :].rearrange("p (h c) -> p h c", h=8), func=AF.Copy), r=["Vt"], w=["Vaug"])
                    if dbg_here:
                        dump("ew", Wt[0:TL_, :], [TL_, 512], "Wt"); dump("a", At[0:TL_, :], [TL_, 512], "At"); dump("kk", KK[0:TL_, :], [TL_, 512], "KK")
                        dump("kh", Kt[0:TL_, :], [TL_, 512], "Kt"); dump("Ein", Ein[0:TL_, :], [TL_, 512], "Ein"); dump("Eex", Eex[0:TL_, :], [TL_, 512], "Eex")
                        dump("EL", EL[0:TL_, :], [TL_, 512], "EL"); dump("Eni", Eni[0:TL_, :], [TL_, 512], "Eni"); dump("bonus", T1[0:TL_, :], [TL_, 512], "T1")
                        dump("Ah", Ah[0:TL_, :], [TL_, 512], "Ah", BF16)
                    for hj in range(8):
                        MM(psb[7][0:64, hj:hj + 1], Ein[0:TL_, hj * 64:(hj + 1) * 64], oh[0:TL_, ohc:ohc + 1], True, True, ["Ein", "oh"], [PK(7)])
                    P.op('dve', lambda e: e.tensor_copy(out=WL[:, hf * 8:(hf + 1) * 8], in_=psb[7][0:64, 0:8]), r=[PK(7)], w=["WL"])
                    for oi, (src_, sk) in enumerate(((Ah, "Ah"), (Bh, "Bh"), (Kh, "Kh"), (Rh, "Rh"))):
                        bk = 3 + oi
                        pv = psb[bk][0:64, :].bitcast(BF16)
                        for hj in range(8):
                            TRP(pv[:, hj * 128:hj * 128 + TL_], src_[0:TL_, hj * 64:(hj + 1) * 64], identb[0:TL_, 0:TL_], [sk, "identb"], [PK(bk)])
                        P.op('act' if oi % 2 else 'dve', (lambda e: e.activation(out=opT[:, oi, :, 0:TL_], in_=pv[:, :].rearrange("p (h t) -> p h t", h=8)[:, :, 0:TL_], func=AF.Copy)) if oi % 2 else
                             (lambda e: e.tensor_copy(out=opT[:, oi, :, 0:TL_], in_=pv[:, :].rearrange("p (h t) -> p h t", h=8)[:, :, 0:TL_])), r=[PK(bk)], w=["opT"])
                    AhT, BhT, KhT, RhT = opT[:, 0], opT[:, 1], opT[:, 2], opT[:, 3]
                    for grp in range(2):
                        hjs = [grp * 4 + j for j in range(4)]
                        v4 = lambda ps_: ps_[0:TL_, :].rearrange("p (j t) -> p j t", j=4)[:, :, 0:TL_]
                        mk3 = lambda i_: msk[0:TL_, i_, 0:TL_].unsqueeze(1).broadcast_to([TL_, 4, TL_])
                        for j, hj in enumerate(hjs):
                            MM(v4(psb[0])[:, j, :], BhT[:, hj, 0:TL_], AhT[:, hj, 0:TL_], True, True, ["opT"], [PK(0)])
                            MM(v4(psb[1])[:, j, :], AhT[:, hj, 0:TL_], BhT[:, hj, 0:TL_], True, True, ["opT"], [PK(1)])
                            MM(v4(psb[2])[:, j, :], KhT[:, hj, 0:TL_], AhT[:, hj, 0:TL_], True, True, ["opT"], [PK(2)])
                            MM(v4(psb[3])[:, j, :], BhT[:, hj, 0:TL_], RhT[:, hj, 0:TL_], True, True, ["opT"], [PK(3)])
                            MM(v4(psb[4])[:, j, :], KhT[:, hj, 0:TL_], RhT[:, hj, 0:TL_], True, True, ["opT"], [PK(4)])
                        TT('dve', PT[0][0:TL_, :, 0:TL_], v4(psb[0]), mk3(3), ALU.mult, [PK(0), "msk"], ["PT0"])
                        TT('dve', Pn[0][0:TL_, :, 0:TL_], v4(psb[1]), mk3(4), ALU.mult, [PK(1), "msk"], ["Pn0"])
                        if TL_ == 128:
                            TT('dve', NoT64[:], v4(psb[0]), mk3(5), ALU.mult, [PK(0), "msk"], ["NoT64"])
                            TT('dve', No64[:], v4(psb[1]), mk3(6), ALU.mult, [PK(1), "msk"], ["No64"])
                            TT('dve', No128[:], v4(psb[1]), mk3(7), ALU.mult, [PK(1), "msk"], ["No128"])
                        TT('dve', NakT[0:TL_, :, 0:TL_], v4(psb[2]), mk3(0), ALU.mult, [PK(2), "msk"], ["NakT"])
                        TT('dve', MrbT[0:TL_, :, 0:TL_], v4(psb[3]), mk3(2), ALU.mult, [PK(3), "msk"], ["MrbT"])
                        TT('dve', MrkT[0:TL_, :, 0:TL_], v4(psb[4]), mk3(2), ALU.mult, [PK(4), "msk"], ["MrkT"])
                        px = psb[5][0:TL_, :].rearrange("p (j i) -> p j i", j=4)
                        for j, hj in enumerate(hjs):
                            hg_ = hf * 8 + hj
                            MM(px[:, j, 0:Wd], AhT[:, hj, 0:TL_], S0Tb[:, hg_, 0:Wd], True, False, ["opT", "S0Tb"], [PK(5)])
                            MM(px[:, j, 0:Wd], NakT[0:TL_, j, 0:TL_], Vaug[0:TL_, hj, 0:Wd], False, True, ["NakT", "Vaug"], [PK(5)])
                        ACTF(Xin[0:TL_, :, 0:Wd], px[:, :, 0:Wd], AF.Copy, [PK(5)], ["Xin"])
                        idb = identb[0:TL_, 0:TL_].unsqueeze(1).broadcast_to([TL_, 4, TL_])
                        TT('dve', TA[0][0:TL_, :, 0:TL_], Pn[0][0:TL_, :, 0:TL_], idb, ALU.add, ["Pn0", "identb"], ["TA0"])
                        TT('dve', TB[0][0:TL_, :, 0:TL_], PT[0][0:TL_, :, 0:TL_], idb, ALU.add, ["PT0", "identb"], ["TB0"])
                        ta, tb = 0, 0
                        for k in range(1, 5):
                            cur, prv = k % 2, (k - 1) % 2
                            for j in range(4):
                                MM(v4(psb[0])[:, j, :], Pn[prv][0:TL_, j, 0:TL_], PT[prv][0:TL_, j, 0:TL_], True, True, ["Pn%d" % prv, "PT%d" % prv], [PK(0)])
                                MM(v4(psb[1])[:, j, :], PT[prv][0:TL_, j, 0:TL_], Pn[prv][0:TL_, j, 0:TL_], True, True, ["Pn%d" % prv, "PT%d" % prv], [PK(1)])
                            ACTF(PT[cur][0:TL_, :, 0:TL_], v4(psb[0]), AF.Copy, [PK(0)], ["PT%d" % cur])
                            P.op('dve', lambda e: e.tensor_copy(out=Pn[cur][0:TL_, :, 0:TL_], in_=v4(psb[1])), r=[PK(1)], w=["Pn%d" % cur])
                            for j in range(4):
                                MM(v4(psb[5])[:, j, :], PT[cur][0:TL_, j, 0:TL_], TA[ta][0:TL_, j, 0:TL_], True, True, ["PT%d" % cur, "TA%d" % ta], [PK(5)])
                                MM(v4(psb[6])[:, j, :], Pn[cur][0:TL_, j, 0:TL_], TB[tb][0:TL_, j, 0:TL_], True, True, ["Pn%d" % cur, "TB%d" % tb], [PK(6)])
                            TT('dve', TA[1 - ta][0:TL_, :, 0:TL_], TA[ta][0:TL_, :, 0:TL_], v4(psb[5]), ALU.add, ["TA%d" % ta, PK(5)], ["TA%d" % (1 - ta)])
                            TT('dve', TB[1 - tb][0:TL_, :, 0:TL_], TB[tb][0:TL_, :, 0:TL_], v4(psb[6]), ALU.add, ["TB%d" % tb, PK(6)], ["TB%d" % (1 - tb)])
                            ta, tb = 1 - ta, 1 - tb
                        if TL_ == 128:
                            for lvl, (NoL, nolk, NoU, nouk) in enumerate(((No64, "No64", NoT64, "NoT64"), (No128, "No128", None, None))):
                                for j in range(4):
                                    MM(v4(psb[0])[:, j, :], NoL[:, j, :], TB[tb][:, j, :], True, True, [nolk, "TB%d" % tb], [PK(0)])
                                ACTF(Zt[0][:], v4(psb[0]), AF.Copy, [PK(0)], ["Zt0"])
                                for j in range(4):
                                    MM(v4(psb[6])[:, j, :], TA[ta][:, j, :], Zt[0][:, j, :], True, True, ["TA%d" % ta, "Zt0"], [PK(6)])
                                if NoU is not None:
                                    for j in range(4):
                                        MM(v4(psb[1])[:, j, :], NoU[:, j, :], TA[ta][:, j, :], True, True, [nouk, "TA%d" % ta], [PK(1)])
                                    P.op('dve', lambda e: e.tensor_copy(out=Zt[1][:], in_=v4(psb[1])), r=[PK(1)], w=["Zt1"])
                                    for j in range(4):
                                        MM(v4(psb[5])[:, j, :], TB[tb][:, j, :], Zt[1][:, j, :], True, True, ["TB%d" % tb, "Zt1"], [PK(5)])
                                    TT('dve', TA[1 - ta][:], TA[ta][:], v4(psb[5]), ALU.add, ["TA%d" % ta, PK(5)], ["TA%d" % (1 - ta)])
                                TT('dve', TB[1 - tb][:], TB[tb][:], v4(psb[6]), ALU.add, ["TB%d" % tb, PK(6)], ["TB%d" % (1 - tb)])
                                if NoU is not None:
                                    ta = 1 - ta
                                tb = 1 - tb
                        pa = psb[6][0:TL_, :].rearrange("p (j i) -> p j i", j=4)
                        for j in range(4):
                            MM(pa[:, j, 0:Wd], TB[tb][0:TL_, j, 0:TL_], Xin[0:TL_, j, 0:Wd], True, True, ["TB%d" % tb, "Xin"], [PK(6)])
                        ACTF(Xb[0:TL_, :, 0:Wd], pa[:, :, 0:Wd], AF.Copy, [PK(6)], ["Xb"])
                        if dbg_here and grp == 0:
                            dump("U", Xb[0:TL_, :, :], [TL_, 4, 128], "Xb", BF16)
                            dump("NakT", NakT[0:TL_, :, 0:TL_], [TL_, 4, TL_], "NakT", BF16)
                            dump("MrkT", MrkT[0:TL_, :, 0:TL_], [TL_, 4, TL_], "MrkT", BF16)
                            dump("opT", opT[:, :, :, 0:TL_], [64, 4, 8, TL_], "opT", BF16)
                        py = psb[7][0:TL_, 0:256].rearrange("p (j i) -> p j i", j=4)
                        for j, hj in enumerate(hjs):
                            hg_ = hf * 8 + hj
                            MM(py[:, j, :], RhT[:, hj, 0:TL_], S0Tb[:, hg_, 0:64], True, False, ["opT", "S0Tb"], [PK(7)])
                            MM(py[:, j, :], MrbT[0:TL_, j, 0:TL_], Xb[0:TL_, j, 0:64], False, False, ["MrbT", "Xb"], [PK(7)])
                            MM(py[:, j, :], MrkT[0:TL_, j, 0:TL_], Vaug[0:TL_, hj, 0:64], False, True, ["MrkT", "Vaug"], [PK(7)])
                        ACTF(yt[0:TL_, grp * 4:(grp + 1) * 4, :], py, AF.Copy, [PK(7)], ["ytm"])
                        if aug:
                            pr_ = psb[2][0:64, :].rearrange("p (j t) -> p j t", j=4)
                            for j, hj in enumerate(hjs):
                                hg_ = hf * 8 + hj
                                MM(pr_[:, j, 0:TL_], S0Tb[:, hg_, 64:128], RhT[:, hj, 0:TL_], True, False, ["opT", "S0Tb"], [PK(2)])
                                MM(pr_[:, j, 0:TL_], Xb[0:TL_, j, 64:128], MrbT[0:TL_, j, 0:TL_], False, True, ["MrbT", "Xb"], [PK(2)])
                            ACTF(rtT[:, grp * 4:(grp + 1) * 4, 0:TL_], pr_[:, :, 0:TL_], AF.Copy, [PK(2)], ["rtT"])
                        pst = psb[3][0:64, :].rearrange("p (j i) -> p j i", j=4)
                        for j, hj in enumerate(hjs):
                            MM(pst[:, j, 0:Wd], Bp[0:TL_, hj * 64:(hj + 1) * 64], Xb[0:TL_, j, 0:Wd], True, False, ["Bp", "Xb"], [PK(3)])
                            MM(pst[:, j, 0:Wd], Kp[0:TL_, hj * 64:(hj + 1) * 64], Vaug[0:TL_, hj, 0:Wd], False, True, ["Kp", "Vaug"], [PK(3)])
                        for j, hj in enumerate(hjs):
                            hg_ = hf * 8 + hj
                            P.op('dve', lambda e: e.scalar_tensor_tensor(out=S0T[:, hg_, 0:Wd], in0=S0T[:, hg_, 0:Wd], scalar=WL[:, hg_:hg_ + 1], in1=pst[:, j, 0:Wd], op0=ALU.mult, op1=ALU.add),
                                 r=["S0T", "WL", PK(3)], w=["S0T"])
                    if dbg_here:
                        dump("yt", yt[0:TL_, :, :], [TL_, 8, 64], "ytm"); dump("WL", WL[:], [64, 16], "WL"); dump("S0T_out", S0T[:], [64, 16, 128], "S0T")
                    P.dma(s_y[rows, cs_], yt[0:TL_, :, :].rearrange("p h c -> p (h c)"), r=["ytm"], w=["s_y"])
                    if aug:
                        P.dma(s_rt[:, hf * 8:(hf + 1) * 8, rows], rtT[:, :, 0:TL_], r=["rtT"], w=["s_rt"])
                ACTF(S0Tb[:], S0T[:], AF.Copy, ["S0T"], ["S0Tb"])

            P.op('dve', lambda e: e.memset(S0T[:], 0.0), w=["S0T"])
            P.op('dve', lambda e: e.tensor_copy(out=S0T[:, :, 64:128], in_=identf[0:64, 0:64].unsqueeze(1).broadcast_to([64, 16, 64])), r=["identf"], w=["S0T"])
            ACTF(S0Tb[:], S0T[:], AF.Copy, ["S0T"], ["S0Tb"])
            for ti in range(TP // 128):
                rec_tile(ti * 128, 128, True)
            P.dma(ag3_in[:, :], S0T[:].rearrange("p h i -> p (h i)"), r=["S0T"], w=["ag3_in"])
            P.op('pool', lambda e: e.collective_compute("AllGather", ALU.bypass, replica_groups=[list(range(NCORES))],
                                                        ins=[ag3_in[:, :]], outs=[ag3_out[:, :]]), r=["ag3_in"], w=["ag3_out"], kind='cc')
            for q in range(NSQ):
                P.dma(sin_[:], st_wkv[m, q].rearrange("h i j -> i h j"), w=["sin_"])
                for g_ in range(2):
                    pv = psb[4 + g_][0:64, :].rearrange("p (h i) -> p h i", h=8)
                    for hh in range(8):
                        TRP(pv[:, hh, :], sin_[:, g_ * 8 + hh, :], identf[0:64, 0:64], ["sin_", "identf"], [PK(4 + g_)])
                    P.op('dve', lambda e: e.tensor_copy(out=S0T[:, g_ * 8:(g_ + 1) * 8, 0:64], in_=pv), r=[PK(4 + g_)], w=["S0T"])
                ACTF(S0Tb[:], S0T[:], AF.Copy, ["S0T"], ["S0Tb"])
                rec_tile(TP + q * TS, TS, False)
                for g_ in range(2):
                    pv = psb[4 + g_][0:64, :].rearrange("p (h i) -> p h i", h=8)
                    for hh in range(8):
                        TRP(pv[:, hh, :], S0T[:, g_ * 8 + hh, 0:64], identf[0:64, 0:64], ["S0T", "identf"], [PK(4 + g_)])
                    P.op('dve', lambda e: e.tensor_copy(out=sot[:, g_ * 8:(g_ + 1) * 8, :], in_=pv), r=[PK(4 + g_)], w=["sot"])
                P.dma(o_wkv_s[m, q].rearrange("h i j -> i h j"), sot[:], r=["sot"], w=["o_wkv_s"])
        P.barrier()
        with contextlib.ExitStack() as ph:
            S = lambda name, shape, dt=F32: ph.enter_context(nc.sbuf_tensor(uname(name), list(shape), dt))
            SsT = S("SsT", [64, 16, 64]); SsTb = S("SsTb", [64, 16, 64], BF16)
            with contextlib.ExitStack() as ph2:
                S2 = lambda name, shape, dt=F32: ph2.enter_context(nc.sbuf_tensor(uname(name), list(shape), dt))
                Lr = S2("Lr", [64, 16, 128]); PTs = S2("PTs", [64, 16, 64]); MT = S2("MT", [64, 16, 64]); sot = S2("sot2", [64, 16, 64])
                P.op('dve', lambda e: e.memset(PTs[:], 0.0), w=["PTs"])
                P.op('dve', lambda e: e.memset(SsT[:], 0.0), w=["SsT"])
                for r_ in range(NCORES):
                    P.dma(Lr[:], ag3_out[r_ * 64:(r_ + 1) * 64, :].rearrange("p (h i) -> p h i", h=16), r=["ag3_out"], w=["Lr"])
                    P.op('dve', lambda e: e.scalar_tensor_tensor(out=SsT[:], in0=PTs[:], scalar=sel[0:64, r_:r_ + 1], in1=SsT[:], op0=ALU.mult, op1=ALU.add), r=["PTs", "sel", "SsT"], w=["SsT"])
                    for g_ in range(2):
                        pv = psb[g_][0:64, :].rearrange("p (h i) -> p h i", h=8)
                        for hh in range(8):
                            TRP(pv[:, hh, :], Lr[:, g_ * 8 + hh, 64:128], identf[0:64, 0:64], ["Lr", "identf"], [PK(g_)])
                        P.op('dve', lambda e: e.tensor_copy(out=MT[:, g_ * 8:(g_ + 1) * 8, :], in_=pv), r=[PK(g_)], w=["MT"])
                    for g_ in range(2):
                        pv = psb[2 + g_][0:64, :].rearrange("p (h i) -> p h i", h=8)
                        for hh in range(8):
                            h_ = g_ * 8 + hh
                            MM(pv[:, hh, :], MT[:, h_, :], PTs[:, h_, :], True, True, ["MT", "PTs"], [PK(2 + g_)])
                    for g_ in range(2):
                        pv = psb[2 + g_][0:64, :].rearrange("p (h i) -> p h i", h=8)
                        TT('dve', PTs[:, g_ * 8:(g_ + 1) * 8, :], pv, Lr[:, g_ * 8:(g_ + 1) * 8, 0:64], ALU.add, [PK(2 + g_), "Lr"], ["PTs"])
                for g_ in range(2):
                    pv = psb[4 + g_][0:64, :].rearrange("p (h i) -> p h i", h=8)
                    for hh in range(8):
                        TRP(pv[:, hh, :], PTs[:, g_ * 8 + hh, :], identf[0:64, 0:64], ["PTs", "identf"], [PK(4 + g_)])
                    P.op('dve', lambda e: e.tensor_copy(out=sot[:, g_ * 8:(g_ + 1) * 8, :], in_=pv), r=[PK(4 + g_)], w=["sot2"])
                P.dma(o_wkv_p[m].rearrange("h i j -> i h j"), sot[:], r=["sot2"], w=["o_wkv_p"])
                ACTF(SsTb[:], SsT[:], AF.Copy, ["SsT"], ["SsTb"])
            P.barrier()
            Wo = S("Wo", [128, 8, 1024], BF16)
            lgbc = S("lgbc", [128, 1024]); lbbc = S("lbbc", [128, 1024])
            Y = S("Y", [128, 1024]); BN = S("BN", [128, 1024]); Q = S("Q", [128, 1024]); Zb = S("Zb", [128, 1024], BF16); VB = S("VB", [128, 1024])
            gF = S("gF", [128, 8, 128], BF16); og = S("ogR", [128, 8, 128], BF16)
            rT = S("rT", [64, 16, 128], BF16)
            mu_ = S("mu_", [128, 16]); m2 = S("m2", [128, 16]); ex2 = S("ex2", [128, 16])
            for cb in range(8):
                load_w(Wo[:, :, cb * 128:(cb + 1) * 128], "Wo", wview(rw_w_out[m], cb * 128, 128), 8, 128)
            P.dma(lgbc[:], rw_ln_g[m].partition_broadcast(128), w=["lgbc"])
            P.dma(lbbc[:], rw_ln_b[m].partition_broadcast(128), w=["lbbc"])
            tiles = [(i * 128, 128) for i in range(TP // 128)] + [(TP, 128)]
            for (t0, n) in tiles:
                is_prompt = t0 < TP
                rows = slice(t0, t0 + n)
                P.dma(Y[:], s_y[rows, :], r=["s_y"], w=["Y"])
                P.dma(BN[:], s_bn[rows, :], r=["s_bn"], w=["BN"])
                P.dma(gF[:], s_g[:, :, rows], r=["s_g"], w=["gF"])
                if is_prompt:
                    P.dma(rT[:], s_rt[:, :, rows], r=["s_rt"], w=["rT"])
                    for g_ in range(2):
                        pv = psb[g_][:, :].rearrange("p (h i) -> p h i", h=8)
                        for hh in range(8):
                            h_ = g_ * 8 + hh
                            MM(pv[:, hh, :], rT[:, h_, :], SsTb[:, h_, :], True, True, ["rT", "SsTb"], [PK(g_)])
                        TT('dve', Y[:, g_ * 512:(g_ + 1) * 512], Y[:, g_ * 512:(g_ + 1) * 512], psb[g_][:, :], ALU.add, ["Y", PK(g_)], ["Y"])
                Y3 = Y[:].rearrange("p (h c) -> p h c", h=16)
                Q3 = Q[:].rearrange("p (h c) -> p h c", h=16)
                VB3 = VB[:].rearrange("p (h c) -> p h c", h=16)
                P.op('dve', lambda e: e.tensor_reduce(out=mu_[:], in_=Y3, axis=mybir.AxisListType.X, op=ALU.add), r=["Y"], w=["mu_"])
                P.op('dve', lambda e: e.tensor_scalar(out=mu_[:], in0=mu_[:], scalar1=1.0 / 64, scalar2=0.0, op0=ALU.mult, op1=ALU.add), r=["mu_"], w=["mu_"])
                TT('dve', Q3, Y3, mu_[:].unsqueeze(2).broadcast_to([128, 16, 64]), ALU.subtract, ["Y", "mu_"], ["Q"])
                ACTF(VB[:], Q[:], AF.Square, ["Q"], ["VB"])
                P.op('dve', lambda e: e.tensor_reduce(out=ex2[:], in_=VB3, axis=mybir.AxisListType.X, op=ALU.add), r=["VB"], w=["ex2"])
                P.op('dve', lambda e: e.tensor_scalar(out=ex2[:], in0=ex2[:], scalar1=1.0 / 64, scalar2=LN_EPS, op0=ALU.mult, op1=ALU.add), r=["ex2"], w=["ex2"])
                P.op('dve', lambda e: e.tensor_copy(out=VB3, in_=ex2[:].unsqueeze(2).broadcast_to([128, 16, 64])), r=["ex2"], w=["VB"])
                ACTF(VB[:], VB[:], AF.Ln, ["VB"], ["VB"])
                ACTF(VB[:], VB[:], AF.Exp, ["VB"], ["VB"], scale=-0.5)
                TT('dve', Q[:], Q[:], VB[:], ALU.mult, ["Q", "VB"], ["Q"])
                TT('dve', Q[:], Q[:], lgbc[:], ALU.mult, ["Q", "lgbc"], ["Q"])
                TT('dve', Q[:], Q[:], lbbc[:], ALU.add, ["Q", "lbbc"], ["Q"])
                TT('dve', Zb[:], Q[:], BN[:], ALU.add, ["Q", "BN"], ["Zb"])
                if debug and l == 1 and t0 == TP:
                    dump("Yin", Y[:], [128, 1024], "Y"); dump("Zb", Zb[:], [128, 1024], "Zb", BF16); dump("ex2", ex2[:], [128, 16], "ex2"); dump("mu_", mu_[:], [128, 16], "mu_")
                for cb in range(8):
                    bk = 4 + cb % 2
                    pv = psb[bk][:, :].bitcast(BF16)
                    TRP(pv[:, 0:128], Zb[:, cb * 128:(cb + 1) * 128], identb[:], ["Zb", "identb"], [PK(bk)])
                    TT('dve', og[:, cb, :], pv[:, 0:128], gF[:, cb, :], ALU.mult, [PK(bk), "gF"], ["ogR"])
                for mo in range(8):
                    bk = 6 + mo % 2
                    for cb in range(8):
                        MM(psb[bk][:, 0:n], Wo[:, cb, mo * 128:(mo + 1) * 128], og[:, cb, 0:n], cb == 0, cb == 7, ["Wo", "ogR"], [PK(bk)])
                    resid_update(psb[bk], PK(bk), mo, t0, n, 16)
                if debug and l == 1 and t0 == TP:
                    dump("ogR", og[:], [128, 8, 128], "ogR", BF16)
                    dump("xafter", x[:, :, TP:TP + 128], [128, 8, 128], "x")
        P.barrier()

    def final_norm():
        with contextlib.ExitStack() as ph:
            yt = ph.enter_context(nc.sbuf_tensor(uname("yt"), [128, 8, 512], F32))
            tiles = [(i * 512, 512) for i in range(4)] + [(TP, 128)]
            for (t0, n) in tiles:
                norm_stats(t0, n, rstd_t[:, 0:n], "rstd_t")
                for c in range(8):
                    P.op('dve', (lambda c_: lambda e: e.scalar_tensor_tensor(out=yt[:, c_, 0:n], in0=x[:, c_, t0:t0 + n], scalar=gfinT[:, c_:c_ + 1], in1=rstd_t[:, 0:n], op0=ALU.mult, op1=ALU.mult))(c),
                         r=["x", "gfinT", "rstd_t"], w=["yt"])
                P.dma(yT[:, :, t0:t0 + n], yt[:, :, 0:n], r=["yt"], w=["yT"])

    try:
        for l in range(DEPTH):
            compute_mod(l)
            if l == 0:
                dump("modv", modv[:], [128, 48, 5], "modv")
            if l % 2 == 0:
                ab_layer(l)
            else:
                rwkv_layer(l)
            mlp_layer(l)
        final_norm()
    except StopBuild:
        pass
    P.emit()
    return nc, es, P


def host_consts(core):
    bf = ml_dtypes.bfloat16
    c = {}
    c["c_identf"] = np.eye(128, dtype=np.float32)
    c["c_identb"] = np.eye(128, dtype=np.float32).astype(bf)
    c["c_onesb"] = np.ones((128, 128), np.float32).astype(bf)
    c["c_ones"] = np.ones((128, 512), np.float32)
    perm = np.zeros((64, 64), np.float32)
    for d in range(32):
        perm[d + 32, d] = -1.0
        perm[d, d + 32] = 1.0
    c["c_perm"] = perm
    pos = np.concatenate([core * TP + np.arange(TP), np.tile(PAST + np.arange(TS), NSQ)]).astype(np.float32)
    inv = (10000.0 ** (-np.arange(32, dtype=np.float32) / 32)).astype(np.float32)
    ang = pos[None, :] * np.concatenate([inv, inv])[:, None]
    c["c_cos"] = np.cos(ang).astype(np.float32)
    c["c_sin"] = np.sin(ang).astype(np.float32)
    g = 1.0 - np.exp2(-5.0 - np.arange(4))
    lg = np.log(g)
    i = np.arange(32)
    m8 = np.zeros((32, 8, 32), np.float64)
    for h in range(4):
        rel = i[None, :] - i[:, None]
        m8[:, h, :] = np.where(rel >= 0, 0.125 * np.exp(rel * lg[h]), 0.0)
        m8[:, 4 + h, :] = (rel >= 0).astype(np.float64)
    c["c_mask8"] = m8.astype(np.float32)
    c["c_gq"] = np.broadcast_to((0.125 * np.exp((i[None, :] + 1.0) * lg[:, None]))[None], (64, 4, 32)).astype(np.float32).copy()
    c["c_gk"] = np.broadcast_to(np.exp((31.0 - i)[:, None] * lg[None, :])[:, :, None], (32, 4, 64)).astype(np.float32).copy()
    c["c_g32"] = np.broadcast_to(np.exp(32.0 * lg)[None, :, None], (64, 4, 128)).astype(np.float32).copy()
    c["c_gn"] = np.broadcast_to(np.exp(32.0 * np.arange(64)[None, :] * lg[:, None])[None], (64, 4, 64)).astype(np.float32).copy()
    c["c_gtot"] = np.broadcast_to(np.exp(float(TP) * lg)[None, :, None], (64, 4, 128)).astype(np.float32).copy()
    ii = np.arange(128)
    msk = np.zeros((128, 8, 128), np.float32)
    rr, cc_ = ii[:, None], ii[None, :]
    msk[:, 3, :] = (cc_ > rr) & (rr // 32 == cc_ // 32)
    msk[:, 4, :] = (cc_ < rr) & (rr // 32 == cc_ // 32)
    msk[:, 5, :] = (rr // 64 == cc_ // 64) & (rr % 64 < 32) & (cc_ % 64 >= 32)
    msk[:, 6, :] = (rr // 64 == cc_ // 64) & (cc_ % 64 < 32) & (rr % 64 >= 32)
    msk[:, 7, :] = (rr >= 64) & (cc_ < 64)
    msk[:, 0, :] = (ii[None, :] > ii[:, None])
    msk[:, 1, :] = (ii[None, :] < ii[:, None])
    msk[:, 2, :] = (ii[None, :] >= ii[:, None])
    c["c_msk"] = msk
    utn = np.zeros((128, 3, 128), np.float32)
    utn[:, 0, :] = -1.0 * (ii[:, None] <= ii[None, :])
    utn[:, 1, :] = -1.0 * (ii[:, None] < ii[None, :])
    utn[:, 2, :] = -1.0 * (ii[:, None] > ii[None, :])
    c["c_utn"] = utn
    oh = np.zeros((128, 2), np.float32); oh[127, 0] = 1.0; oh[31, 1] = 1.0
    c["c_oh"] = oh
    sel = np.zeros((128, 8), np.float32); sel[:, core] = 1.0
    selp = np.zeros((128, 8), np.float32)
    if core > 0:
        selp[:, core - 1] = 1.0
    c["c_sel"] = sel
    c["c_selp"] = selp
    return c


_CACHE = {}


def kernel(**inp):
    f32 = np.float32
    dbg = inp.pop("_debug", None)
    if "prog" not in _CACHE:
        _CACHE["prog"] = build_program(dbg)
    nc, es, P = _CACHE["prog"]
    xp = np.asarray(inp["x_prompt"], f32)[0]
    xs = np.asarray(inp["x_sample"], f32)
    shared = {
        "mod_w": np.asarray(inp["mod_w"], f32),
        "mod_b": np.asarray(inp["mod_b"], f32).reshape(192, 128),
        "norm_mix_g": np.asarray(inp["norm_mix_g"], f32).reshape(32, 128),
        "norm_mlp_g": np.asarray(inp["norm_mlp_g"], f32).reshape(32, 128),
        "final_g": np.asarray(inp["final_g"], f32).reshape(8, 128),
        "mlp_w1": np.asarray(inp["mlp_w1"], f32),
        "mlp_w2": np.asarray(inp["mlp_w2"], f32),
        "ab_w_in": np.asarray(inp["ab_w_in"], f32),
        "ab_w_out": np.asarray(inp["ab_w_out"], f32),
        "hg_lb": np.asarray(inp["hg_lb"], f32).reshape(8, 128),
        "hg_norm_g": np.asarray(inp["hg_norm_g"], f32).reshape(2, 128),
        "rw_mu": np.asarray(inp["rw_mu"], f32).reshape(96, 128),
        "rw_r_k": np.asarray(inp["rw_r_k"], f32).reshape(2, 1024),
    }
    for k_ in ("rw_w_rkv", "rw_w0", "rw_w1", "rw_w2", "rw_a0", "rw_a1", "rw_a2", "rw_v0", "rw_v1", "rw_v2", "rw_g1", "rw_g2",
               "rw_k_k", "rw_k_a", "rw_ln_g", "rw_ln_b", "rw_w_out"):
        shared[k_] = np.asarray(inp[k_], f32)
    in_maps = []
    for c in range(NCORES):
        toks = np.concatenate([xp[c * TP:(c + 1) * TP], xs[c * NSQ:(c + 1) * NSQ].reshape(NSQ * TS, D)], 0)
        xTc = np.ascontiguousarray(toks.T.reshape(8, 128, T).transpose(1, 0, 2))
        cond = np.concatenate([np.asarray(inp["c_prompt"], f32), np.asarray(inp["c_sample"], f32)[c * NSQ:(c + 1) * NSQ]], 0)
        condTc = np.ascontiguousarray(cond.T.reshape(8, 128, 5).transpose(1, 0, 2))
        d = dict(shared)
        d.update(host_consts(c))
        d["xT"] = xTc
        d["condT"] = condTc
        d["st_ret"] = np.ascontiguousarray(np.asarray(inp["state_ret"], f32)[:, c * NSQ:(c + 1) * NSQ])
        d["st_hg"] = np.ascontiguousarray(np.asarray(inp["state_hgrn"], f32)[:, c * NSQ:(c + 1) * NSQ])
        d["st_wkv"] = np.ascontiguousarray(np.asarray(inp["state_wkv"], f32)[:, c * NSQ:(c + 1) * NSQ])
        d["st_shT"] = np.ascontiguousarray(np.asarray(inp["state_shift"], f32)[:, c * NSQ:(c + 1) * NSQ].reshape(2, NSQ, 8, 128))
        in_maps.append(d)
    in_maps = [{k: v for k, v in d.items() if k in P.in_names} for d in in_maps]
    res = run_bass_kernel_spmd(nc, in_maps, core_ids=list(range(NCORES)))
    R = res.results
    if dbg:
        return [{k: np.asarray(R[c][k]) for k in P.dumps} for c in range(NCORES)]
    y_p = np.zeros((1, NCORES * TP, D), f32)
    y_s = np.zeros((NCORES * NSQ, TS, D), f32)
    for c in range(NCORES):
        yt = np.asarray(R[c]["yT"]).transpose(1, 0, 2).reshape(D, T).T
        y_p[0, c * TP:(c + 1) * TP] = yt[:TP]
        y_s[c * NSQ:(c + 1) * NSQ] = yt[TP:].reshape(NSQ, TS, D)
    cat = lambda k: np.concatenate([np.asarray(R[c][k]) for c in range(NCORES)], 1)
    ret_p = np.asarray(R[0]["o_ret_p"])[:, None]
    hg_p = np.asarray(R[0]["o_hg_p"])[:, None]
    wkv_p = np.asarray(R[0]["o_wkv_p"])[:, None]
    sh_p = np.asarray(R[NCORES - 1]["o_sh_p"]).transpose(0, 2, 1).reshape(2, 1, D)
    ret_s = cat("o_ret_s"); hg_s = cat("o_hg_s"); wkv_s = cat("o_wkv_s")
    sh_s = cat("o_sh_s").transpose(0, 1, 3, 2).reshape(2, NCORES * NSQ, D)
    out = (y_p, y_s, ret_p, ret_s, hg_p, hg_s, wkv_p, wkv_s, sh_p, sh_s)
    return tuple(np.ascontiguousarray(o.astype(f32)) for o in out)
```
